# Optimizing a Trainium2 kernel written in Bass

```python
import math
import jax, jax.numpy as jnp
from jax import lax
import numpy as np

D_MODEL = 1024
BATCH = 8
SEQ = 4096
DEPTH = 2

GRID_W = 64
EPS = 1e-6
ROPE_THETA = 10000.0
Q_BLOCK = 128

MLA_HEADS = 6
MLA_NOPE = 64
MLA_ROPE = 32
MLA_V = 64
MLA_QK = MLA_NOPE + MLA_ROPE
Q_LORA = 256
KV_LORA = 128
MLA_WIDTH = MLA_HEADS * MLA_V

NA_HEADS = 6
NA_DIM = 64
NA_WIDTH = NA_HEADS * NA_DIM
NA_KR_MAX = 8
NA_KC = 16

DIFF_HEADS = 4
DIFF_QK = 32
DIFF_V = 2 * DIFF_QK
DIFF_WIDTH = DIFF_HEADS * DIFF_V

MIX_WIDTH = MLA_WIDTH + NA_WIDTH + DIFF_WIDTH

IN_SIZES = (Q_LORA, KV_LORA, MLA_ROPE, 3 * NA_WIDTH, 3 * DIFF_WIDTH, MLA_WIDTH, NA_WIDTH, DIFF_WIDTH)
N_IN = sum(IN_SIZES)
IN_SPLITS = tuple(sum(IN_SIZES[:i + 1]) for i in range(len(IN_SIZES) - 1))

kernel_name = "hybrid_mla_natten_diff_encoder"


def rms_norm(x, g):
    xf = x.astype(jnp.float32)
    y = xf * lax.rsqrt(jnp.mean(xf * xf, axis=-1, keepdims=True) + EPS)
    return (y * g.astype(jnp.float32)).astype(x.dtype)


def rope_tables(seq_len, dim):
    inv = ROPE_THETA ** (-jnp.arange(0, dim, 2, dtype=jnp.float32) / dim)
    ang = jnp.arange(seq_len, dtype=jnp.float32)[:, None] * inv[None, :]
    return jnp.cos(ang), jnp.sin(ang)


def apply_rope(x, cos, sin):
    half = x.shape[-1] // 2
    shape = (1, x.shape[1]) + (1,) * (x.ndim - 3) + (half,)
    cs, sn = cos.reshape(shape), sin.reshape(shape)
    xf = x.astype(jnp.float32)
    x1, x2 = xf[..., :half], xf[..., half:]
    return jnp.concatenate([x1 * cs - x2 * sn, x2 * cs + x1 * sn], axis=-1).astype(x.dtype)


def blocked_attention(q, k, v, scale):
    B, S, H, d = q.shape
    dv = v.shape[-1]
    nb = S // Q_BLOCK
    qb = q.reshape(B, nb, Q_BLOCK, H, d).transpose(1, 0, 2, 3, 4)

    def one(qi):
        s = jnp.einsum('bqhd,bkhd->bhqk', qi, k, preferred_element_type=jnp.float32) * scale
        p = jax.nn.softmax(s, axis=-1)
        return jnp.einsum('bhqk,bkhe->bqhe', p.astype(v.dtype), v)

    out = lax.map(one, qb)
    return out.transpose(1, 0, 2, 3, 4).reshape(B, S, H, dv)


def blocked_diff_attention(q, k, v, lam, scale):
    B, S, H, _, d = q.shape
    dv = v.shape[-1]
    nb = S // Q_BLOCK
    qb = q.reshape(B, nb, Q_BLOCK, H, 2, d).transpose(1, 0, 2, 3, 4, 5)

    def one(qi):
        s = jnp.einsum('bqhnd,bkhnd->bhnqk', qi, k, preferred_element_type=jnp.float32) * scale
        p = jax.nn.softmax(s, axis=-1)
        a = p[:, :, 0] - lam * p[:, :, 1]
        return jnp.einsum('bhqk,bkhe->bqhe', a.astype(v.dtype), v)

    out = lax.map(one, qb)
    return out.transpose(1, 0, 2, 3, 4).reshape(B, S, H, dv)


def neighborhood_attention(q, k, v, rpb, scale):
    B, S, H, d = q.shape
    rows = S // GRID_W
    kr = min(NA_KR_MAX, rows)
    kc = NA_KC
    qg = q.reshape(B, rows, GRID_W, H, d)
    kg = k.reshape(B, rows, GRID_W, H, d)
    vg = v.reshape(B, rows, GRID_W, H, d)
    col = jnp.arange(GRID_W)
    col_start = jnp.clip(col - kc // 2, 0, GRID_W - kc)
    col_idx = col_start[:, None] + jnp.arange(kc)[None, :]
    dc = col_idx - col[:, None] + (NA_KC - 1)

    def one(r):
        r0 = jnp.clip(r - kr // 2, 0, rows - kr)
        q_r = lax.dynamic_index_in_dim(qg, r, axis=1, keepdims=False)
        k_r = lax.dynamic_slice_in_dim(kg, r0, kr, axis=1)[:, :, col_idx]
        v_r = lax.dynamic_slice_in_dim(vg, r0, kr, axis=1)[:, :, col_idx]
        dr = r0 + jnp.arange(kr) - r + (NA_KR_MAX - 1)
        bias = rpb[:, dr[:, None, None], dc[None, :, :]]
        s = jnp.einsum('bchd,brcjhd->bhcrj', q_r, k_r, preferred_element_type=jnp.float32) * scale
        s = s + bias.transpose(0, 2, 1, 3)[None].astype(jnp.float32)
        p = jax.nn.softmax(s.reshape(B, H, GRID_W, kr * kc), axis=-1).reshape(B, H, GRID_W, kr, kc)
        return jnp.einsum('bhcrj,brcjhd->bchd', p.astype(v.dtype), v_r)

    out = lax.map(one, jnp.arange(rows))
    return out.transpose(1, 0, 2, 3, 4).reshape(B, S, H, d)


def hybrid_layer(x, c, layer_idx, ada_w, ada_b, norm_g, w_in, q_lat_g, w_uq, kv_lat_g, w_ukv,
                 mla_q_g, mla_k_g, na_q_g, na_k_g, na_rpb, diff_q_g, diff_k_g,
                 lam_q1, lam_k1, lam_q2, lam_k2, subln_g, w_out,
                 cos_mla, sin_mla, cos_diff, sin_diff):
    B, S, _ = x.shape
    mod = jax.nn.silu(c) @ ada_w + ada_b
    shift, scale, gate = jnp.split(mod, 3, axis=-1)
    h = rms_norm(x, norm_g) * (1.0 + scale[:, None, :]) + shift[:, None, :]

    proj = h @ w_in
    c_q, c_kv, k_pe, na_qkv, diff_qkv, g_mla, g_na, g_diff = jnp.split(proj, IN_SPLITS, axis=-1)

    q = (rms_norm(c_q, q_lat_g) @ w_uq).reshape(B, S, MLA_HEADS, MLA_QK)
    kv = (rms_norm(c_kv, kv_lat_g) @ w_ukv).reshape(B, S, MLA_HEADS, MLA_NOPE + MLA_V)
    k_nope, v_mla = kv[..., :MLA_NOPE], kv[..., MLA_NOPE:]
    k = jnp.concatenate([k_nope, jnp.broadcast_to(k_pe[:, :, None, :], (B, S, MLA_HEADS, MLA_ROPE))], axis=-1)
    q = rms_norm(q, mla_q_g)
    k = rms_norm(k, mla_k_g)
    q = jnp.concatenate([q[..., :MLA_NOPE], apply_rope(q[..., MLA_NOPE:], cos_mla, sin_mla)], axis=-1)
    k = jnp.concatenate([k[..., :MLA_NOPE], apply_rope(k[..., MLA_NOPE:], cos_mla, sin_mla)], axis=-1)
    o_mla = blocked_attention(q, k, v_mla, MLA_QK ** -0.5).reshape(B, S, MLA_WIDTH)

    qn, kn, vn = jnp.split(na_qkv, 3, axis=-1)
    qn = rms_norm(qn.reshape(B, S, NA_HEADS, NA_DIM), na_q_g)
    kn = rms_norm(kn.reshape(B, S, NA_HEADS, NA_DIM), na_k_g)
    vn = vn.reshape(B, S, NA_HEADS, NA_DIM)
    o_na = neighborhood_attention(qn, kn, vn, na_rpb, NA_DIM ** -0.5).reshape(B, S, NA_WIDTH)

    qd, kd, vd = jnp.split(diff_qkv, 3, axis=-1)
    qd = apply_rope(rms_norm(qd.reshape(B, S, DIFF_HEADS, 2, DIFF_QK), diff_q_g), cos_diff, sin_diff)
    kd = apply_rope(rms_norm(kd.reshape(B, S, DIFF_HEADS, 2, DIFF_QK), diff_k_g), cos_diff, sin_diff)
    vd = vd.reshape(B, S, DIFF_HEADS, DIFF_V)
    lam_init = 0.8 - 0.6 * math.exp(-0.3 * layer_idx)
    f32 = jnp.float32
    lam = (jnp.exp(jnp.sum(lam_q1.astype(f32) * lam_k1.astype(f32)))
           - jnp.exp(jnp.sum(lam_q2.astype(f32) * lam_k2.astype(f32))) + lam_init)
    o_d = blocked_diff_attention(qd, kd, vd, lam, DIFF_QK ** -0.5)
    o_d = (rms_norm(o_d, subln_g) * (1.0 - lam_init)).reshape(B, S, DIFF_WIDTH)

    y = jnp.concatenate([o_mla * jax.nn.silu(g_mla),
                         o_na * jax.nn.silu(g_na),
                         o_d * jax.nn.silu(g_diff)], axis=-1) @ w_out
    return x + gate[:, None, :] * y


def setup_inputs(seed: int = 0) -> dict:
    key = jax.random.key(seed)
    ks = jax.random.split(key, 24)
    f32 = jnp.float32
    L = DEPTH

    def nrm(k, shape, s):
        return jax.random.normal(k, shape, f32) * s

    def gain(k, shape):
        return 1.0 + 0.05 * jax.random.normal(k, shape, f32)

    return {
        "x": jax.random.normal(ks[0], (BATCH, SEQ, D_MODEL), f32),
        "c": jax.random.normal(ks[1], (BATCH, D_MODEL), f32),
        "ada_w": nrm(ks[2], (L, D_MODEL, 3 * D_MODEL), 0.5 * D_MODEL ** -0.5),
        "ada_b": nrm(ks[3], (L, 3 * D_MODEL), 0.02),
        "norm_g": gain(ks[4], (L, D_MODEL)),
        "w_in": nrm(ks[5], (L, D_MODEL, N_IN), D_MODEL ** -0.5),
        "q_lat_g": gain(ks[6], (L, Q_LORA)),
        "w_uq": nrm(ks[7], (L, Q_LORA, MLA_HEADS * MLA_QK), Q_LORA ** -0.5),
        "kv_lat_g": gain(ks[8], (L, KV_LORA)),
        "w_ukv": nrm(ks[9], (L, KV_LORA, MLA_HEADS * (MLA_NOPE + MLA_V)), KV_LORA ** -0.5),
        "mla_q_g": gain(ks[10], (L, MLA_QK)),
        "mla_k_g": gain(ks[11], (L, MLA_QK)),
        "na_q_g": gain(ks[12], (L, NA_DIM)),
        "na_k_g": gain(ks[13], (L, NA_DIM)),
        "na_rpb": nrm(ks[14], (L, NA_HEADS, 2 * NA_KR_MAX - 1, 2 * NA_KC - 1), 0.1),
        "diff_q_g": gain(ks[15], (L, DIFF_QK)),
        "diff_k_g": gain(ks[16], (L, DIFF_QK)),
        "lam_q1": nrm(ks[17], (L, DIFF_QK), 0.1),
        "lam_k1": nrm(ks[18], (L, DIFF_QK), 0.1),
        "lam_q2": nrm(ks[19], (L, DIFF_QK), 0.1),
        "lam_k2": nrm(ks[20], (L, DIFF_QK), 0.1),
        "subln_g": gain(ks[21], (L, DIFF_V)),
        "w_out": nrm(ks[22], (L, MIX_WIDTH, D_MODEL), MIX_WIDTH ** -0.5),
    }


def reference(x, c, ada_w, ada_b, norm_g, w_in, q_lat_g, w_uq, kv_lat_g, w_ukv,
              mla_q_g, mla_k_g, na_q_g, na_k_g, na_rpb, diff_q_g, diff_k_g,
              lam_q1, lam_k1, lam_q2, lam_k2, subln_g, w_out):
    S = x.shape[1]
    cos_mla, sin_mla = rope_tables(S, MLA_ROPE)
    cos_diff, sin_diff = rope_tables(S, DIFF_QK)
    h = x
    for i in range(DEPTH):
        h = hybrid_layer(h, c, i, ada_w[i], ada_b[i], norm_g[i], w_in[i], q_lat_g[i], w_uq[i],
                         kv_lat_g[i], w_ukv[i], mla_q_g[i], mla_k_g[i], na_q_g[i], na_k_g[i],
                         na_rpb[i], diff_q_g[i], diff_k_g[i], lam_q1[i], lam_k1[i], lam_q2[i],
                         lam_k2[i], subln_g[i], w_out[i], cos_mla, sin_mla, cos_diff, sin_diff)
    return h
```

```python
import numpy as np
import ml_dtypes
import concourse.bass as bass
import concourse.mybir as mybir
from concourse.bass_utils import run_bass_kernel_spmd

F32 = mybir.dt.float32
BF16 = mybir.dt.bfloat16
AF = mybir.ActivationFunctionType
ALU = mybir.AluOpType

ENG_NAMES = ("pe", "act", "dve", "pool", "sp")


class View:
    def __init__(self, tile, ap):
        self.tile = tile
        self.ap = ap

    def __getitem__(self, idx):
        return View(self.tile, self.ap[idx])

    def rearrange(self, pat, **kw):
        return View(self.tile, self.ap.rearrange(pat, **kw))


class Tile:
    def __init__(self, name, t):
        self.name = name
        self.t = t
        self.writes = {}
        self.reads = {}

    def __getitem__(self, idx):
        return View(self, self.t[idx])


def _tiles(views):
    out = []
    for v in views:
        if isinstance(v, View) and v.tile not in out:
            out.append(v.tile)
    return out


def _ap(v):
    return v.ap if isinstance(v, View) else v


class Sched:
    def __init__(self, nc):
        self.nc = nc
        self.eng = {"pe": nc.tensor, "act": nc.scalar, "dve": nc.vector, "pool": nc.gpsimd, "sp": nc.sync}
        self.sem = {e: nc.alloc_semaphore("prog_" + e) for e in ("pe", "act", "dve", "pool")}
        self.cnt = {e: 0 for e in self.sem}
        self.known = {e: {} for e in ENG_NAMES}
        self.dma_sems = {}
        self.n_wait = 0
        self.n_inst = 0
        self.scopes = []
        self.uid = 0
        self.all_tiles = []
        self.epoch = 0

    def push(self):
        self.scopes.append([])

    def pop(self):
        self.barrier()
        for g in reversed(self.scopes.pop()):
            g.__exit__(None, None, None)

    def sbuf(self, name, shape, dtype):
        self.uid += 1
        g = self.nc.sbuf_tensor("sb_%s_%d" % (name, self.uid), list(shape), dtype)
        t = g.__enter__()
        self.scopes[-1].append(g)
        tl = Tile(name, t)
        self.all_tiles.append(tl)
        return tl

    def psum(self, name, shape, dtype=F32):
        self.uid += 1
        g = self.nc.psum_tensor("ps_%s_%d" % (name, self.uid), list(shape), dtype)
        t = g.__enter__()
        self.scopes[-1].append(g)
        tl = Tile(name, t)
        self.all_tiles.append(tl)
        return tl

    def _need(self, e, k, val, out):
        kind, key = k
        if kind == "eng" and key == e and e == "pe":
            return
        kn = self.known[e]
        if kn.get(k, 0) >= val:
            return
        sem = self.sem[key] if kind == "eng" else self.dma_sems[key][0]
        out.append((sem, val))
        self.n_wait += 1
        kn[k] = val

    def _wait(self, e, k, val):
        out = []
        self._need(e, k, val, out)
        for sem, v in out:
            self.eng[e].wait_ge(sem, v)

    def _deps(self, e, reads, writes, attach=False):
        out = []
        for t in reads:
            for k, v in t.writes.items():
                self._need(e, k, v, out)
        for t in writes:
            for k, v in t.writes.items():
                self._need(e, k, v, out)
            for k, v in t.reads.items():
                self._need(e, k, v, out)
        last = out.pop() if (attach and out) else None
        for sem, v in out:
            self.eng[e].wait_ge(sem, v)
        return last

    def _record(self, ev, reads, writes):
        for t in reads:
            if t in writes:
                continue
            k = (ev[0], ev[1])
            if t.reads.get(k, 0) < ev[2]:
                t.reads[k] = ev[2]
        for t in writes:
            t.writes = {(ev[0], ev[1]): ev[2]}
            t.reads = {}

    def op(self, e, fn, reads=(), writes=()):
        rt, wt = _tiles(reads), _tiles(writes)
        last = self._deps(e, rt, wt, attach=True)
        inst = fn(self.eng[e])
        if last is not None:
            inst._wait_ge(*last)
        self.cnt[e] += 1
        self.n_inst += 1
        inst.then_inc(self.sem[e], 1)
        self._record(("eng", e, self.cnt[e]), rt, wt)

    def act(self, out, in_, func, bias=None, scale=1.0, accum=None):
        kw = {}
        if bias is not None:
            kw["bias"] = _ap(bias)
        if accum is not None:
            kw["accum_out"] = _ap(accum)
        self.op("act", lambda e: e.activation(out.ap, in_.ap, func, scale=_ap(scale), **kw),
                reads=[in_, bias, scale], writes=[out, accum])

    def tt(self, e, out, a, b, op):
        self.op(e, lambda g: g.tensor_tensor(out.ap, a.ap, b.ap, op), reads=[a, b], writes=[out])

    def stt(self, e, out, in0, scalar, in1, op0, op1):
        self.op(e, lambda g: g.scalar_tensor_tensor(out.ap, in0.ap, _ap(scalar), in1.ap, op0, op1),
                reads=[in0, scalar, in1], writes=[out])

    def ts(self, e, out, in0, s1, s2, op0, op1=None):
        if op1 is None:
            self.op(e, lambda g: g.tensor_scalar(out.ap, in0.ap, _ap(s1), None, op0), reads=[in0, s1], writes=[out])
        else:
            self.op(e, lambda g: g.tensor_scalar(out.ap, in0.ap, _ap(s1), _ap(s2), op0, op1),
                    reads=[in0, s1, s2], writes=[out])

    def copy(self, e, out, in_):
        if e == "act":
            self.op(e, lambda g: g.copy(out.ap, in_.ap), reads=[in_], writes=[out])
        else:
            self.op(e, lambda g: g.tensor_copy(out.ap, in_.ap), reads=[in_], writes=[out])

    def recip(self, out, in_):
        self.op("dve", lambda g: g.reciprocal(out.ap, in_.ap), reads=[in_], writes=[out])

    def memset(self, e, out, val):
        self.op(e, lambda g: g.memset(out.ap, val), writes=[out])

    def reduce_sum(self, out, in_):
        self.op("dve", lambda g: g.tensor_reduce(out.ap, in_.ap, mybir.AxisListType.X, ALU.add), reads=[in_], writes=[out])

    def mm(self, items):
        reads, writes = [], []
        for it in items:
            writes.append(it[0])
            reads += [it[1], it[2]]
        rt, wt = _tiles(reads), _tiles(writes)
        last = self._deps("pe", rt, wt, attach=True)
        inst = None
        for it in items:
            kw = {}
            if len(it) > 5 and it[5] is not None:
                kw["tile_position"] = it[5]
            inst = self.eng["pe"].matmul(it[0].ap, it[1].ap, it[2].ap, start=it[3], stop=it[4], **kw)
            if last is not None:
                inst._wait_ge(*last)
                last = None
            self.n_inst += 1
        self.cnt["pe"] += 1
        inst.then_inc(self.sem["pe"], 1)
        self._record(("eng", "pe", self.cnt["pe"]), rt, wt)

    def transposes(self, items, ident):
        reads = [ident] + [it[1] for it in items]
        writes = [it[0] for it in items]
        rt, wt = _tiles(reads), _tiles(writes)
        last = self._deps("pe", rt, wt, attach=True)
        inst = None
        for it in items:
            inst = self.eng["pe"].transpose(it[0].ap, it[1].ap, ident.ap)
            if last is not None:
                inst._wait_ge(*last)
                last = None
            self.n_inst += 1
        self.cnt["pe"] += 1
        inst.then_inc(self.sem["pe"], 1)
        self._record(("eng", "pe", self.cnt["pe"]), rt, wt)

    def dma(self, q, out, in_, **kw):
        rt, wt = _tiles([in_]), _tiles([out])
        self._deps(q, rt, wt)
        semname = (wt[0].name if wt else rt[0].name)
        if semname not in self.dma_sems:
            self.dma_sems[semname] = [self.nc.alloc_semaphore("d_" + semname), 0]
        ds = self.dma_sems[semname]
        ds[1] += 16
        self.eng[q].dma_start(out=_ap(out), in_=_ap(in_), **kw).then_inc(ds[0], 16)
        self.n_inst += 1
        self._record(("dma", semname, ds[1]), rt, wt)

    def _all_events(self):
        evs = [(("eng", k), v) for k, v in self.cnt.items() if v > 0]
        evs += [(("dma", k), v[1]) for k, v in self.dma_sems.items() if v[1] > 0]
        return evs

    def barrier(self, engines=ENG_NAMES):
        for e in engines:
            for k, v in self._all_events():
                self._wait(e, k, v)

    def new_epoch(self):
        self.barrier()
        self.epoch += 1
        self.sem = {e: self.nc.alloc_semaphore("prog%d_%s" % (self.epoch, e)) for e in ("pe", "act", "dve", "pool")}
        self.cnt = {e: 0 for e in self.sem}
        for e in ENG_NAMES:
            for k in [k for k in self.known[e] if k[0] == "eng"]:
                del self.known[e][k]
        for tl in self.all_tiles:
            tl.writes = {k: v for k, v in tl.writes.items() if k[0] != "eng"}
            tl.reads = {k: v for k, v in tl.reads.items() if k[0] != "eng"}

    def finish(self):
        self.barrier(engines=("sp",))


class Rot:
    def __init__(self, S, name, shape, dtype, n, psum=False):
        self.tiles = [(S.psum if psum else S.sbuf)("%s%d" % (name, i), shape, dtype) for i in range(n)]
        self.i = 0

    def next(self):
        t = self.tiles[self.i % len(self.tiles)]
        self.i += 1
        return t


D = 1024
SEQ = 4096
BATCH = 8
DEPTH = 2
EPS = 1e-6
NT = SEQ // 512
NCH = SEQ // 128
N_IN = 3360
N_INX = 3904
C_CQ, C_CKV, C_KPE, C_NAQ, C_NAK, C_NAV = 0, 256, 384, 416, 800, 1184
C_DQ, C_DK, C_DV, C_G = 1568, 1824, 2080, 2336
C_KPEP, C_DQP, C_DKP = 3360, 3392, 3648
NG = 16
MASK = -4000.0
NAB_TILES = 126
PERM_M = np.concatenate([np.arange(64), np.arange(80, 96), np.arange(64, 80)])
PERM_32 = np.concatenate([np.arange(16, 32), np.arange(0, 16)])


def _lam_init(l):
    return 0.8 - 0.6 * float(np.exp(-0.3 * l))


def _rope_tables():
    def tab(dim):
        inv = (np.float32(10000.0) ** (-(np.arange(0, dim, 2, dtype=np.float32)) / np.float32(dim))).astype(np.float32)
        ang = (np.arange(SEQ, dtype=np.float32)[:, None] * inv[None, :]).astype(np.float32)
        return np.cos(ang.astype(np.float64)).astype(np.float32).T, np.sin(ang.astype(np.float64)).astype(np.float32).T
    cm, sm = tab(32)
    cosm = np.ones((96, SEQ), np.float32)
    sinm = np.zeros((96, SEQ), np.float32)
    cosm[64:80] = cm
    cosm[80:96] = cm
    sinm[64:80] = -sm
    sinm[80:96] = sm
    cd, sd = tab(32)
    cosd = np.tile(cd, (8, 1))
    sind = np.tile(np.concatenate([-sd, sd], 0), (4, 1))
    return cosm, sinm, cosd.astype(np.float32), sind.astype(np.float32)


def _na_chunks(j):
    if j <= 1:
        return [0, 1, 2, 3]
    if j >= 30:
        return [28, 29, 30, 31]
    return [j - 2, j - 1, j, j + 1, j + 2]


def _nab_index(h, j, i):
    if 2 <= j <= 29:
        return h * 5 + (i - j + 2)
    jb = j if j <= 1 else j - 28
    ii = i if j <= 1 else i - 28
    return 30 + (h * 4 + jb) * 4 + ii


def _na_bias_tables(rpb):
    out = np.full((128, NAB_TILES, 128), MASK, np.float32)
    a = np.arange(2)[:, None]
    cc = np.arange(64)[None, :]
    pairs = [(10, 10 + d) for d in range(-2, 3)] + [(j, i) for j in (0, 1, 30, 31) for i in _na_chunks(j)]
    for (j, i) in pairs:
        rk = (2 * i + a + 0 * cc).reshape(128)
        ck = (0 * a + cc).reshape(128)
        rq = (2 * j + a + 0 * cc).reshape(128)
        cq = ck
        r0 = np.clip(rq - 4, 0, 56)
        c0 = np.clip(cq - 8, 0, 48)
        vr = (rk[:, None] >= r0[None, :]) & (rk[:, None] < r0[None, :] + 8)
        vc = (ck[:, None] >= c0[None, :]) & (ck[:, None] < c0[None, :] + 16)
        valid = vr & vc
        dr = np.clip(rk[:, None] - rq[None, :] + 7, 0, 14)
        dc = np.clip(ck[:, None] - cq[None, :] + 15, 0, 30)
        for h in range(6):
            t = np.where(valid, rpb[h][dr, dc], np.float32(MASK)).astype(np.float32)
            out[:, _nab_index(h, j, i), :] = t
    return out


def _prep_layer_inputs(inp):
    L = DEPTH
    f32 = np.float32
    o = {}
    o["ada_w_r"] = np.ascontiguousarray(inp["ada_w"].reshape(L, 8, 128, 3 * D).transpose(0, 2, 1, 3))
    o["ada_b"] = np.ascontiguousarray(inp["ada_b"].reshape(L, 1, 3 * D))
    o["norm_g"] = np.ascontiguousarray(inp["norm_g"])
    w_in = inp["w_in"]
    ext = np.concatenate([
        w_in,
        w_in[:, :, C_KPE + PERM_32],
        w_in[:, :, C_DQ + (np.arange(256) // 32) * 32 + PERM_32[np.arange(256) % 32]],
        w_in[:, :, C_DK + (np.arange(256) // 32) * 32 + PERM_32[np.arange(256) % 32]],
    ], axis=2)
    o["w_in_r"] = np.ascontiguousarray(ext.reshape(L, 8, 128, N_INX).transpose(0, 2, 1, 3))
    w_uq = inp["w_uq"]
    permq = (np.arange(576) // 96) * 96 + PERM_M[np.arange(576) % 96]
    uq = np.concatenate([w_uq, w_uq[:, :, permq]], axis=2)
    o["w_uq_r"] = np.ascontiguousarray(uq.reshape(L, 2, 128, 1152).transpose(0, 2, 1, 3))
    w_ukv = inp["w_ukv"].reshape(L, 128, 6, 128)
    wk = np.zeros((L, 128, 6, 96), f32)
    wk[:, :, :, 0:64] = w_ukv[:, :, :, 0:64]
    o["wk_r"] = wk
    o["wv_r"] = np.ascontiguousarray(w_ukv[:, :, :, 64:128].reshape(L, 128, 384))
    o["w_out_r"] = np.ascontiguousarray(inp["w_out"].reshape(L, 8, 128, D).transpose(0, 2, 1, 3))
    g = np.zeros((L, 128, NG), f32)
    g[:, :, 0] = inp["q_lat_g"][:, 0:128]
    g[:, :, 1] = inp["q_lat_g"][:, 128:256]
    g[:, :, 2] = inp["kv_lat_g"]
    g[:, 0:96, 3] = inp["mla_q_g"]
    g[:, 0:96, 4] = inp["mla_q_g"][:, PERM_M]
    g[:, 0:96, 5] = inp["mla_k_g"]
    g[:, 0:96, 6] = inp["mla_k_g"][:, PERM_M]
    g[:, :, 7] = np.tile(inp["na_q_g"], (1, 2))
    g[:, :, 8] = np.tile(inp["na_k_g"], (1, 2))
    g[:, :, 9] = np.tile(inp["diff_q_g"], (1, 4))
    g[:, :, 10] = np.tile(inp["diff_q_g"][:, PERM_32], (1, 4))
    g[:, :, 11] = np.tile(inp["diff_k_g"], (1, 4))
    g[:, :, 12] = np.tile(inp["diff_k_g"][:, PERM_32], (1, 4))
    g[:, 0:64, 13] = inp["subln_g"]
    o["gains"] = g
    o["lamv"] = np.ascontiguousarray(np.stack([inp["lam_q1"], inp["lam_k1"], inp["lam_q2"], inp["lam_k2"]], axis=1).reshape(L, 128))
    o["nab"] = np.stack([_na_bias_tables(inp["na_rpb"][l]) for l in range(L)], axis=0)
    return {k: np.ascontiguousarray(v, dtype=f32) for k, v in o.items()}


def _const_inputs():
    cosm, sinm, cosd, sind = _rope_tables()
    bf = ml_dtypes.bfloat16
    esel = np.zeros((32, 96), np.float32)
    esel[np.arange(32), 64 + np.arange(32)] = 1.0
    bd32 = np.kron(np.eye(4, dtype=np.float32), np.ones((32, 32), np.float32))
    bd64 = np.kron(np.eye(2, dtype=np.float32), np.ones((64, 64), np.float32))
    return {
        "cosm": cosm, "sinm": sinm, "cosd": cosd, "sind": sind,
        "ident": np.eye(128, dtype=np.float32).astype(bf),
        "esel": esel.astype(bf), "bd32": bd32.astype(bf), "bd64": bd64.astype(bf),
    }


def _dram_in(nc, name, shape, dtype=F32):
    return nc.dram_tensor(name, list(shape), dtype, kind="ExternalInput").ap()


def build_program(layers, n_layers_total=DEPTH):
    nc = bass.Bass("TRN2", target_bir_lowering=False)
    S = Sched(nc)
    L = n_layers_total
    x_in = _dram_in(nc, "x", [SEQ, D])
    c_r = _dram_in(nc, "c_r", [128, 8])
    W = {
        "ada_w_r": _dram_in(nc, "ada_w_r", [L, 128, 8, 3 * D]),
        "ada_b": _dram_in(nc, "ada_b", [L, 1, 3 * D]),
        "norm_g": _dram_in(nc, "norm_g", [L, D]),
        "w_in_r": _dram_in(nc, "w_in_r", [L, 128, 8, N_INX]),
        "w_uq_r": _dram_in(nc, "w_uq_r", [L, 128, 2, 1152]),
        "wk_r": _dram_in(nc, "wk_r", [L, 128, 6, 96]),
        "wv_r": _dram_in(nc, "wv_r", [L, 128, 384]),
        "w_out_r": _dram_in(nc, "w_out_r", [L, 128, 8, D]),
        "gains": _dram_in(nc, "gains", [L, 128, NG]),
        "lamv": _dram_in(nc, "lamv", [L, 128]),
        "nab": _dram_in(nc, "nab", [L, 128, NAB_TILES, 128]),
        "cosm": _dram_in(nc, "cosm", [96, SEQ]),
        "sinm": _dram_in(nc, "sinm", [96, SEQ]),
        "cosd": _dram_in(nc, "cosd", [128, SEQ]),
        "sind": _dram_in(nc, "sind", [128, SEQ]),
        "ident": _dram_in(nc, "ident", [128, 128], BF16),
        "esel": _dram_in(nc, "esel", [32, 96], BF16),
        "bd32": _dram_in(nc, "bd32", [128, 128], BF16),
        "bd64": _dram_in(nc, "bd64", [128, 128], BF16),
    }
    y_out = nc.dram_tensor("y", [SEQ, D], F32, kind="ExternalOutput").ap()
    scr = {
        "qT_mla": nc.dram_tensor("s_qT_mla", [6, 96, SEQ], BF16).ap(),
        "kT_mla": nc.dram_tensor("s_kT_mla", [6, 96, SEQ], BF16).ap(),
        "v_mla": nc.dram_tensor("s_v_mla", [SEQ, 390], BF16).ap(),
        "qT_na": nc.dram_tensor("s_qT_na", [3, 128, SEQ], BF16).ap(),
        "kT_na": nc.dram_tensor("s_kT_na", [3, 128, SEQ], BF16).ap(),
        "v_na": nc.dram_tensor("s_v_na", [SEQ, 390], BF16).ap(),
        "qT_d": nc.dram_tensor("s_qT_d", [2, 128, SEQ], BF16).ap(),
        "kT_d": nc.dram_tensor("s_kT_d", [2, 128, SEQ], BF16).ap(),
        "v_d": nc.dram_tensor("s_v_d", [SEQ, 260], BF16).ap(),
        "sgT": nc.dram_tensor("s_sgT", [D, SEQ], BF16).ap(),
        "mixT": nc.dram_tensor("s_mixT", [D, SEQ], BF16).ap(),
    }
    xmid = [nc.dram_tensor("s_x%d" % i, [SEQ, D], F32).ap() for i in range(max(0, len(layers) - 1))]

    S.push()
    K = {}
    K["ident"] = S.sbuf("ident", [128, 128], BF16)
    K["esel"] = S.sbuf("esel", [32, 96], BF16)
    K["bd32"] = S.sbuf("bd32", [128, 128], BF16)
    K["bd64"] = S.sbuf("bd64", [128, 128], BF16)
    K["ones_b"] = S.sbuf("ones_b", [128, 128], BF16)
    K["ones_f"] = S.sbuf("ones_f", [128, 128], F32)
    K["eps"] = S.sbuf("eps", [128, 1], F32)
    for nm in ("ident", "esel", "bd32", "bd64"):
        S.dma("sp", K[nm][:], W[nm][:, :])
    S.memset("dve", K["ones_b"][:], 1.0)
    S.memset("dve", K["ones_f"][:], 1.0)
    S.memset("dve", K["eps"][:], EPS)
    S.memset("pool", K["eps"][:], EPS)

    for li, l in enumerate(layers):
        xin = x_in if li == 0 else xmid[li - 1]
        xout = y_out if li == len(layers) - 1 else xmid[li]
        if li > 0:
            S.new_epoch()
        _layer(nc, S, K, W, scr, l, xin, xout, c_r)
    S.finish()
    S.scopes.pop()
    return nc, S


def _layer(nc, S, K, W, scr, l, xin, xout, c_r):
    S.push()
    gains = S.sbuf("gains", [128, NG], F32)
    nlam = S.sbuf("nlam", [128, 1], F32)
    woutg = S.sbuf("woutg", [128, 8, D], BF16)
    S.dma("sp", gains[:], W["gains"][l])

    S.push()
    Gb = S.sbuf("Gb", [128, D], F32)
    Shb = S.sbuf("Shb", [128, D], F32)
    Wf = S.sbuf("Wf", [128, 8, N_INX], BF16)
    Wuq = S.sbuf("Wuq", [128, 2, 1152], BF16)
    Wk = S.sbuf("Wk", [128, 6, 96], BF16)
    Wv = S.sbuf("Wv", [128, 384], BF16)

    S.push()
    PS0 = Rot(S, "ps0_", [128, 512], F32, 2, psum=True)
    c_sb = S.sbuf("c_sb", [128, 8], F32)
    sc = S.sbuf("sc", [128, 8], F32)
    screp = S.sbuf("screp", [128, 8, 128], F32)
    adab = S.sbuf("adab", [1, 3 * D], F32)
    modb = S.sbuf("modb", [128, 3 * D], F32)
    ngb = S.sbuf("ngb", [128, D], F32)
    lamb = S.sbuf("lamb", [128, 128], F32)
    lt = S.sbuf("lt", [128, 64], F32)
    ls = S.sbuf("ls", [128, 2], F32)
    le = S.sbuf("le", [128, 2], F32)
    STG = Rot(S, "stg", [128, 4096], F32, 2)
    S.dma("sp", c_sb[:], c_r[:, :])
    S.dma("sp", adab[:], W["ada_b"][l])
    S.dma("sp", ngb[:], W["norm_g"][l].partition_broadcast(128))
    S.dma("sp", lamb[:], W["lamv"][l].partition_broadcast(128))
    S.act(sc[:], c_sb[:], AF.Silu)
    for k in range(8):
        S.ts("dve", screp[:, k, :], K["ones_f"][:], sc[:, k:k + 1], None, ALU.mult)
    for n in range(6):
        st = STG.next()
        S.dma("sp", st[:].rearrange("p (k n) -> p k n", k=8), W["ada_w_r"][l][:, :, n * 512:(n + 1) * 512])
        ps = PS0.next()
        stv = st[:].rearrange("p (k n) -> p k n", k=8)
        items = [(ps[:], screp[:, k, :], stv[:, k, :], k == 0, False) for k in range(8)]
        items.append((ps[:], K["ones_f"][0:1, :], adab[0:1, n * 512:(n + 1) * 512], False, True))
        S.mm(items)
        S.copy("dve", modb[:, n * 512:(n + 1) * 512], ps[:])
    S.copy("dve", Shb[:], modb[:, 0:D])
    S.stt("dve", Gb[:], modb[:, D:2 * D], 1.0, ngb[:], ALU.add, ALU.mult)
    S.tt("dve", lt[:, 0:32], lamb[:, 0:32], lamb[:, 32:64], ALU.mult)
    S.tt("dve", lt[:, 32:64], lamb[:, 64:96], lamb[:, 96:128], ALU.mult)
    S.reduce_sum(ls[:, 0:1], lt[:, 0:32])
    S.reduce_sum(ls[:, 1:2], lt[:, 32:64])
    S.act(le[:], ls[:], AF.Exp)
    S.tt("dve", nlam[:], le[:, 1:2], le[:, 0:1], ALU.subtract)
    S.ts("dve", nlam[:], nlam[:], -_lam_init(l), None, ALU.add)
    S.ts("dve", gains[:, 13:14], gains[:, 13:14], 1.0 - _lam_init(l), None, ALU.mult)
    for k in range(8):
        st = STG.next()
        S.dma("sp", st[:, 0:N_INX], W["w_in_r"][l][:, k, :])
        S.copy("dve" if k % 2 == 0 else "pool", Wf[:, k, :], st[:, 0:N_INX])
    st = STG.next()
    S.dma("sp", st[:, 0:2304].rearrange("p (k n) -> p k n", k=2), W["w_uq_r"][l])
    S.copy("dve", Wuq[:], st[:, 0:2304].rearrange("p (k n) -> p k n", k=2))
    st = STG.next()
    S.dma("sp", st[:, 0:576].rearrange("p (h n) -> p h n", h=6), W["wk_r"][l])
    S.dma("sp", st[:, 1024:1408], W["wv_r"][l])
    S.copy("dve", Wk[:], st[:, 0:576].rearrange("p (h n) -> p h n", h=6))
    S.copy("dve", Wv[:], st[:, 1024:1408])
    for half in range(2):
        st = STG.next()
        S.dma("sp", st[:].rearrange("p (c d) -> p c d", c=4), W["w_out_r"][l][:, half * 4:(half + 1) * 4, :])
        for c in range(4):
            S.tt("dve" if c % 2 == 0 else "pool", woutg[:, half * 4 + c, :], st[:, c * D:(c + 1) * D], modb[:, 2 * D:3 * D], ALU.mult)
    S.pop()

    _phase_a(nc, S, K, W, scr, l, xin, gains, Gb, Shb, Wf, Wuq, Wk, Wv)
    S.pop()

    _phase_attn(nc, S, K, W, scr, l, gains, nlam)
    _phase_c(nc, S, K, scr, xin, xout, woutg)
    S.pop()


def _phase_a(nc, S, K, W, scr, l, xin, gains, Gb, Shb, Wf, Wuq, Wk, Wv):
    S.push()
    XT = Rot(S, "xt", [128, D], F32, 2)
    junk = S.sbuf("junk", [128, D], BF16)
    st1 = Rot(S, "st1_", [128, 4], F32, 2)
    hn = S.sbuf("hn", [128, D], F32)
    hb = Rot(S, "hb", [128, D], BF16, 2)
    HT = Rot(S, "hT", [128, 8, 512], BF16, 2)
    COSM = Rot(S, "cosm", [96, 512], F32, 2)
    SINM = Rot(S, "sinm", [96, 512], F32, 2)
    COSD = Rot(S, "cosd", [128, 512], F32, 2)
    SIND = Rot(S, "sind", [128, 512], F32, 2)
    SQ = Rot(S, "sq", [128, 512], BF16, 3)
    SD = Rot(S, "sd", [128, 512], F32, 2)
    RS = Rot(S, "rs", [128, 512], F32, 2)
    TA = Rot(S, "ta", [128, 512], F32, 2)
    TB = Rot(S, "tb", [128, 512], F32, 2)
    TC = Rot(S, "tc", [128, 512], F32, 2)
    OB = Rot(S, "ob", [128, 512], BF16, 4)
    cqn = S.sbuf("cqn", [128, 2, 512], BF16)
    ckvn = S.sbuf("ckvn", [128, 512], BF16)
    kpe = S.sbuf("kpe", [32, 2, 512], BF16)
    VS = Rot(S, "vs", [128, 6, 65], BF16, 3)
    psT = S.psum("psT", [128, D], BF16)
    PA = Rot(S, "pa", [128, 512], F32, 3, psum=True)
    PB = Rot(S, "pb", [128, 512], F32, 2, psum=True)
    PSS = Rot(S, "pss", [128, 512], F32, 2, psum=True)
    for t in VS.tiles:
        S.memset("pool", t[:], 1.0)

    def rstd_from(ps_list, M, ones, n):
        sqs = []
        for ps in ps_list:
            sq = SQ.next()
            S.act(sq[0:M, :], ps[0:M, :], AF.Square)
            sqs.append(sq)
        pss = PSS.next()
        S.mm([(pss[0:M, :], ones[0:M, 0:M], sq[0:M, :], i == 0, i == len(sqs) - 1) for i, sq in enumerate(sqs)])
        sd = SD.next()
        S.act(sd[0:M, :], pss[0:M, :], AF.Sqrt, bias=K["eps"][0:M, 0:1], scale=1.0 / n)
        rs = RS.next()
        S.recip(rs[0:M, :], sd[0:M, :])
        return rs

    def proj(cols, M, ht):
        ps = PA.next()
        S.mm([(ps[0:M, :], Wf[:, k, cols:cols + M], ht[:, k, :], k == 0, k == 7) for k in range(8)])
        return ps

    def projb(cols, M, ht):
        ps = PB.next()
        S.mm([(ps[0:M, :], Wf[:, k, cols:cols + M], ht[:, k, :], k == 0, k == 7) for k in range(8)])
        return ps

    def rope_finish(ps, psp, M, gcol, gpcol, cos, sin, rs, dst):
        ta, tb, tc = TA.next(), TB.next(), TC.next()
        S.stt("dve", ta[0:M, :], ps[0:M, :], gains[0:M, gcol:gcol + 1], cos[0:M, :], ALU.mult, ALU.mult)
        S.stt("dve", tb[0:M, :], psp[0:M, :], gains[0:M, gpcol:gpcol + 1], sin[0:M, :], ALU.mult, ALU.mult)
        S.tt("pool", tc[0:M, :], ta[0:M, :], tb[0:M, :], ALU.add)
        ob = OB.next()
        S.tt("pool", ob[0:M, :], tc[0:M, :], rs[0:M, :], ALU.mult)
        S.dma("pool", dst, ob[0:M, :])

    def load_x(T, s):
        xt = XT.next()
        r0 = T * 512 + s * 128
        S.dma("sp", xt[:], xin[r0:r0 + 128, :])
        return xt

    xq = [load_x(0, 0)]
    for T in range(NT):
        t0 = T * 512
        cosm, sinm, cosd, sind = COSM.next(), SINM.next(), COSD.next(), SIND.next()
        S.dma("sp", cosm[:], W["cosm"][:, t0:t0 + 512])
        S.dma("sp", sinm[:], W["sinm"][:, t0:t0 + 512])
        S.dma("sp", cosd[:], W["cosd"][:, t0:t0 + 512])
        S.dma("sp", sind[:], W["sind"][:, t0:t0 + 512])
        ht = HT.next()
        for s in range(4):
            xt = xq.pop(0)
            nxt = (T, s + 1) if s < 3 else (T + 1, 0)
            if nxt[0] < NT:
                xq.append(load_x(*nxt))
            st = st1.next()
            S.act(junk[:], xt[:], AF.Square, accum=st[:, 0:1])
            S.act(st[:, 1:2], st[:, 0:1], AF.Sqrt, bias=K["eps"][:, 0:1], scale=1.0 / D)
            S.recip(st[:, 2:3], st[:, 1:2])
            S.stt("dve", hn[:], xt[:], st[:, 2:3], Gb[:], ALU.mult, ALU.mult)
            h = hb.next()
            S.tt("pool", h[:], hn[:], Shb[:], ALU.add)
            S.transposes([(psT[:, k * 128:(k + 1) * 128], h[:, k * 128:(k + 1) * 128]) for k in range(8)], K["ident"][:])
            S.copy("dve", ht[:, :, s * 128:(s + 1) * 128], psT[:].rearrange("p (k t) -> p k t", k=8))

        pc = [proj(C_CQ, 128, ht), proj(C_CQ + 128, 128, ht)]
        rs = rstd_from(pc, 128, K["ones_b"], 256)
        for k in range(2):
            S.stt("dve", cqn[:, k, :], pc[k][:], gains[:, k:k + 1], rs[:], ALU.mult, ALU.mult)
        for h in range(6):
            ps, psp = PA.next(), PB.next()
            S.mm([(ps[0:96, :], Wuq[:, k, h * 96:(h + 1) * 96], cqn[:, k, :], k == 0, k == 1) for k in range(2)])
            S.mm([(psp[0:96, :], Wuq[:, k, 576 + h * 96:576 + (h + 1) * 96], cqn[:, k, :], k == 0, k == 1) for k in range(2)])
            rs = rstd_from([ps], 96, K["ones_b"], 96)
            rope_finish(ps, psp, 96, 3, 4, cosm, sinm, rs, scr["qT_mla"][h, :, t0:t0 + 512])

        pk = proj(C_CKV, 128, ht)
        rs = rstd_from([pk], 128, K["ones_b"], 128)
        S.stt("dve", ckvn[:], pk[:], gains[:, 2:3], rs[:], ALU.mult, ALU.mult)
        pe1 = proj(C_KPE, 32, ht)
        S.copy("act", kpe[:, 0, :], pe1[0:32, :])
        pe2 = projb(C_KPEP, 32, ht)
        S.copy("act", kpe[:, 1, :], pe2[0:32, :])
        for h in range(6):
            ps, psp = PA.next(), PB.next()
            S.mm([(ps[0:96, :], Wk[:, h, :], ckvn[:], True, False), (ps[0:96, :], K["esel"][:], kpe[:, 0, :], False, True)])
            S.mm([(psp[0:96, :], Wk[:, h, :], ckvn[:], True, False), (psp[0:96, :], K["esel"][:], kpe[:, 1, :], False, True)])
            rs = rstd_from([ps], 96, K["ones_b"], 96)
            rope_finish(ps, psp, 96, 5, 6, cosm, sinm, rs, scr["kT_mla"][h, :, t0:t0 + 512])
        for s in range(4):
            ps = PA.next()
            S.mm([(ps[:, 0:384], ckvn[:, s * 128:(s + 1) * 128], Wv[:], True, True)])
            vs = VS.next()
            S.copy("dve", vs[:, :, 0:64], ps[:, 0:384].rearrange("p (h d) -> p h d", h=6))
            S.dma("pool", scr["v_mla"][t0 + s * 128:t0 + (s + 1) * 128, :], vs[:].rearrange("p h d -> p (h d)"))

        for (c0, gcol, dst) in ((C_NAQ, 7, "qT_na"), (C_NAK, 8, "kT_na")):
            for c in range(3):
                ps = proj(c0 + c * 128, 128, ht)
                rs = rstd_from([ps], 128, K["bd64"], 64)
                ob = OB.next()
                S.stt("dve", ob[:], ps[:], gains[:, gcol:gcol + 1], rs[:], ALU.mult, ALU.mult)
                S.dma("pool", scr[dst][c, :, t0:t0 + 512], ob[:])
        for s in range(4):
            ps = PA.next()
            S.mm([(ps[:, 0:384], ht[:, k, s * 128:(s + 1) * 128], Wf[:, k, C_NAV:C_NAV + 384], k == 0, k == 7) for k in range(8)])
            vs = VS.next()
            S.copy("dve", vs[:, :, 0:64], ps[:, 0:384].rearrange("p (h d) -> p h d", h=6))
            S.dma("pool", scr["v_na"][t0 + s * 128:t0 + (s + 1) * 128, :], vs[:].rearrange("p h d -> p (h d)"))

        for (c0, cp, gcol, dst) in ((C_DQ, C_DQP, 9, "qT_d"), (C_DK, C_DKP, 11, "kT_d")):
            for c in range(2):
                ps = proj(c0 + c * 128, 128, ht)
                psp = projb(cp + c * 128, 128, ht)
                rs = rstd_from([ps], 128, K["bd32"], 32)
                rope_finish(ps, psp, 128, gcol, gcol + 1, cosd, sind, rs, scr[dst][c, :, t0:t0 + 512])
        for s in range(4):
            ps = PA.next()
            S.mm([(ps[:, 0:256], ht[:, k, s * 128:(s + 1) * 128], Wf[:, k, C_DV:C_DV + 256], k == 0, k == 7) for k in range(8)])
            vs = VS.next()
            S.copy("dve", vs[:, 0:4, 0:64], ps[:, 0:256].rearrange("p (h d) -> p h d", h=4))
            S.dma("pool", scr["v_d"][t0 + s * 128:t0 + (s + 1) * 128, :], vs[:, 0:4, :].rearrange("p h d -> p (h d)"))

        for c in range(8):
            ps = proj(C_G + c * 128, 128, ht)
            ob = OB.next()
            S.act(ob[:], ps[:], AF.Silu)
            S.dma("pool", scr["sgT"][c * 128:(c + 1) * 128, t0:t0 + 512], ob[:])
    S.pop()


def _phase_attn(nc, S, K, W, scr, l, gains, nlam):
    S.push()
    PSS = Rot(S, "psS", [128, 1024], F32, 2, psum=True)
    PSO = Rot(S, "psO", [128, 512], F32, 2, psum=True)
    psB = S.psum("psB", [128, 512], F32)
    psN = S.psum("psN", [128, 512], F32)
    PT = Rot(S, "pt", [128, 1024], BF16, 2)
    OS = Rot(S, "osb", [128, 512], F32, 2)
    rrow = S.sbuf("rrow", [128, 512], F32)
    UU = Rot(S, "uu", [64, 512], F32, 3)
    SG = Rot(S, "sg", [64, 512], BF16, 2)
    MX = Rot(S, "mx", [64, 512], BF16, 2)

    def epilogue(pso):
        osb = OS.next()
        S.copy("dve", osb[0:65, :], pso[0:65, :])
        S.recip(rrow[64:65, :], osb[64:65, :])
        S.mm([(psB[0:64, :], K["ones_f"][64:65, 0:64], rrow[64:65, :], True, True)])
        u = UU.next()
        S.tt("dve", u[:], osb[0:64, :], psB[0:64, :], ALU.mult)
        return u

    def load_sg(row0, t0):
        sg = SG.next()
        S.dma("sp", sg[:], scr["sgT"][row0:row0 + 64, t0:t0 + 512])
        return sg

    def store_mix(u_like, sg, row0, t0, eng="pool"):
        mx = MX.next()
        S.tt(eng, mx[:], u_like[:], sg[:], ALU.mult)
        S.dma("pool", scr["mixT"][row0:row0 + 64, t0:t0 + 512], mx[:])

    def dense_unit(kt, qt, vfn, scale, tp):
        pso = PSO.next()
        NP = NCH // 2
        cur = {}

        def emit_s(j):
            pss = PSS.next()
            cur[j] = pss
            S.mm([(pss[:, 0:512], kt(2 * j), qt, True, True, tp), (pss[:, 512:1024], kt(2 * j + 1), qt, True, True, tp)])

        emit_s(0)
        for j in range(NP):
            if j + 1 < NP:
                emit_s(j + 1)
            p = PT.next()
            S.act(p[:], cur.pop(j)[:], AF.Exp, scale=scale)
            S.mm([(pso[0:65, :], vfn(2 * j), p[:, 0:512], j == 0, False),
                  (pso[0:65, :], vfn(2 * j + 1), p[:, 512:1024], False, j == NP - 1)])
        return epilogue(pso)

    S.push()
    Vt = S.sbuf("Vt", [128, NCH, 390], BF16)
    KT = Rot(S, "KT", [96, SEQ], BF16, 2)
    QT = Rot(S, "QT", [96, 512], BF16, 2)
    S.dma("sp", Vt[:], scr["v_mla"].rearrange("(c p) f -> p c f", p=128))
    for h in range(6):
        kt = KT.next()
        S.dma("sp", kt[:], scr["kT_mla"][h])
        for T in range(NT):
            t0 = T * 512
            qt = QT.next()
            S.dma("sp", qt[:], scr["qT_mla"][h, :, t0:t0 + 512])
            sg = load_sg(h * 64, t0)
            u = dense_unit(lambda kc, kt=kt: kt[:, kc * 128:(kc + 1) * 128], qt[:],
                           lambda kc, h=h: Vt[:, kc, h * 65:(h + 1) * 65], 96.0 ** -0.5, None)
            store_mix(u, sg, h * 64, t0)
    S.pop()

    S.push()
    Vd = S.sbuf("Vd", [128, NCH, 260], BF16)
    KTd = Rot(S, "KTd", [128, SEQ], BF16, 2)
    QTd = Rot(S, "QTd", [128, 512], BF16, 2)
    dd = S.sbuf("dd", [64, 512], F32)
    dsq = S.sbuf("dsq", [64, 512], BF16)
    dsd = S.sbuf("dsd", [64, 512], F32)
    drs = S.sbuf("drs", [64, 512], F32)
    dm = S.sbuf("dm", [64, 512], F32)
    S.dma("sp", Vd[:], scr["v_d"].rearrange("(c p) f -> p c f", p=128))
    for c in range(2):
        kt = KTd.next()
        S.dma("sp", kt[:], scr["kT_d"][c])
        for T in range(NT):
            t0 = T * 512
            qt = QTd.next()
            S.dma("sp", qt[:], scr["qT_d"][c, :, t0:t0 + 512])
            for hh in range(2):
                h = 2 * c + hh
                sg = load_sg(768 + h * 64, t0)
                us = []
                for n in range(2):
                    b = (hh * 2 + n) * 32
                    tp = (b, 0) if b == 96 else None
                    us.append(dense_unit(lambda kc, kt=kt, b=b: kt[b:b + 32, kc * 128:(kc + 1) * 128], qt[b:b + 32, :],
                                         lambda kc, h=h: Vd[:, kc, h * 65:(h + 1) * 65], 32.0 ** -0.5, tp))
                S.stt("dve", dd[:], us[1][:], nlam[0:64, 0:1], us[0][:], ALU.mult, ALU.add)
                S.tt("pool", dsq[:], dd[:], dd[:], ALU.mult)
                S.mm([(psN[0:64, :], K["ones_b"][0:64, 0:64], dsq[:], True, True)])
                S.act(dsd[:], psN[0:64, :], AF.Sqrt, bias=K["eps"][0:64, 0:1], scale=1.0 / 64)
                S.recip(drs[:], dsd[:])
                S.stt("dve", dm[:], dd[:], gains[0:64, 13:14], drs[:], ALU.mult, ALU.mult)
                store_mix(dm, sg, 768 + h * 64, t0)
    S.pop()

    S.push()
    Vn = S.sbuf("Vn", [128, NCH, 390], BF16)
    KTn = S.sbuf("KTn", [128, 3, SEQ], BF16)
    QTn = S.sbuf("QTn", [128, 3, SEQ], BF16)
    nabb = S.sbuf("nabb", [128, NAB_TILES, 128], BF16)
    NST = Rot(S, "nst", [128, 14, 128], F32, 2)
    S.dma("sp", Vn[:], scr["v_na"].rearrange("(c p) f -> p c f", p=128))
    for c in range(3):
        S.dma("sp", KTn[:, c, :], scr["kT_na"][c])
        S.dma("sp", QTn[:, c, :], scr["qT_na"][c])
    for g in range(NAB_TILES // 14):
        st = NST.next()
        S.dma("sp", st[:], W["nab"][l][:, g * 14:(g + 1) * 14, :])
        S.ts("dve" if g % 2 == 0 else "pool", nabb[:, g * 14:(g + 1) * 14, :], st[:], 8.0, None, ALU.mult)
    for J in range(NT):
        t0 = J * 512
        for h in range(6):
            c, b = h // 2, (h % 2) * 64
            sg = load_sg(384 + h * 64, t0)
            pso = PSO.next()
            subs = []
            for jj in range(4):
                j = 4 * J + jj
                subs.append((jj, j, _na_chunks(j)))
            cur = {}

            def emit_s(idx):
                jj, j, chunks = subs[idx]
                pss = PSS.next()
                cur[idx] = pss
                items = []
                for ci, i in enumerate(chunks):
                    o = pss[:, ci * 128:(ci + 1) * 128]
                    items.append((o, KTn[b:b + 64, c, i * 128:(i + 1) * 128], QTn[b:b + 64, c, j * 128:(j + 1) * 128], True, False))
                    items.append((o, K["ident"][:], nabb[:, _nab_index(h, j, i), :], False, True))
                S.mm(items)

            emit_s(0)
            for idx in range(4):
                if idx + 1 < 4:
                    emit_s(idx + 1)
                jj, j, chunks = subs[idx]
                n = len(chunks)
                p = PT.next()
                S.act(p[:, 0:n * 128], cur.pop(idx)[:, 0:n * 128], AF.Exp, scale=0.125)
                S.mm([(pso[0:65, jj * 128:(jj + 1) * 128], Vn[:, i, h * 65:(h + 1) * 65], p[:, ci * 128:(ci + 1) * 128], ci == 0, ci == n - 1)
                      for ci, i in enumerate(chunks)])
            u = epilogue(pso)
            store_mix(u, sg, 384 + h * 64, t0)
    S.pop()
    S.pop()


def _phase_c(nc, S, K, scr, xin, xout, woutg):
    S.push()
    MT = Rot(S, "mt", [128, 8, 512], BF16, 2)
    XC = Rot(S, "xc", [128, D], F32, 2)
    OC = Rot(S, "oc", [128, D], F32, 2)
    PY = Rot(S, "py", [128, 512], F32, 4, psum=True)
    for T in range(NT):
        t0 = T * 512
        mt = MT.next()
        S.dma("sp", mt[:], scr["mixT"][:, t0:t0 + 512].rearrange("(c p) t -> p c t", p=128))
        for s in range(4):
            r0 = t0 + s * 128
            xc = XC.next()
            S.dma("sp", xc[:], xin[r0:r0 + 128, :])
            oc = OC.next()
            for n in range(2):
                py = PY.next()
                S.mm([(py[:], mt[:, c, s * 128:(s + 1) * 128], woutg[:, c, n * 512:(n + 1) * 512], c == 0, c == 7) for c in range(8)])
                S.tt("dve", oc[:, n * 512:(n + 1) * 512], py[:], xc[:, n * 512:(n + 1) * 512], ALU.add)
            S.dma("pool", xout[r0:r0 + 128, :], oc[:])
    S.pop()


_PROG_CACHE = {}


def _get_program(layers):
    key = tuple(layers)
    if key not in _PROG_CACHE:
        _PROG_CACHE[key] = build_program(list(layers))[0]
    return _PROG_CACHE[key]


FUSED = True


def kernel(**inputs):
    inp = {k: np.asarray(v, dtype=np.float32) for k, v in inputs.items()}
    shared = _prep_layer_inputs(inp)
    shared.update(_const_inputs())
    x = inp["x"]
    c = inp["c"]
    c_r = [np.ascontiguousarray(c[b].reshape(8, 128).T) for b in range(BATCH)]
    plan = [list(range(DEPTH))] if FUSED else [[l] for l in range(DEPTH)]
    cur = [np.ascontiguousarray(x[b]) for b in range(BATCH)]
    for layers in plan:
        nc = _get_program(layers)
        in_maps = []
        for b in range(BATCH):
            m = dict(shared)
            m["x"] = cur[b]
            m["c_r"] = c_r[b]
            in_maps.append(m)
        res = run_bass_kernel_spmd(nc, in_maps, core_ids=list(range(BATCH)))
        cur = [np.asarray(res.results[b]["y"], dtype=np.float32) for b in range(BATCH)]
    return np.stack(cur, axis=0)
```

```python
import numpy as np
import ml_dtypes
import concourse.bass as bass
import concourse.mybir as mybir
from concourse.bass_utils import run_bass_kernel_spmd

F32 = mybir.dt.float32
BF16 = mybir.dt.bfloat16
AF = mybir.ActivationFunctionType
ALU = mybir.AluOpType

ENG_NAMES = ("pe", "act", "dve", "pool", "sp")


class View:
    def __init__(self, tile, ap):
        self.tile = tile
        self.ap = ap

    def __getitem__(self, idx):
        return View(self.tile, self.ap[idx])

    def rearrange(self, pat, **kw):
        return View(self.tile, self.ap.rearrange(pat, **kw))


class Tile:
    def __init__(self, name, t):
        self.name = name
        self.t = t
        self.writes = {}
        self.reads = {}

    def __getitem__(self, idx):
        return View(self, self.t[idx])


def _tiles(views):
    out = []
    for v in views:
        if isinstance(v, View) and v.tile not in out:
            out.append(v.tile)
    return out


def _ap(v):
    return v.ap if isinstance(v, View) else v


class Sched:
    def __init__(self, nc):
        self.nc = nc
        self.eng = {"pe": nc.tensor, "act": nc.scalar, "dve": nc.vector, "pool": nc.gpsimd, "sp": nc.sync}
        self.sem = {e: nc.alloc_semaphore("prog_" + e) for e in ("pe", "act", "dve", "pool")}
        self.cnt = {e: 0 for e in self.sem}
        self.known = {e: {} for e in ENG_NAMES}
        self.dma_sems = {}
        self.n_wait = 0
        self.n_inst = 0
        self.scopes = []
        self.uid = 0
        self.all_tiles = []
        self.epoch = 0

    def push(self):
        self.scopes.append([])

    def pop(self):
        self.barrier()
        for g in reversed(self.scopes.pop()):
            g.__exit__(None, None, None)

    def sbuf(self, name, shape, dtype):
        self.uid += 1
        g = self.nc.sbuf_tensor("sb_%s_%d" % (name, self.uid), list(shape), dtype)
        t = g.__enter__()
        self.scopes[-1].append(g)
        tl = Tile(name, t)
        self.all_tiles.append(tl)
        return tl

    def psum(self, name, shape, dtype=F32):
        self.uid += 1
        g = self.nc.psum_tensor("ps_%s_%d" % (name, self.uid), list(shape), dtype)
        t = g.__enter__()
        self.scopes[-1].append(g)
        tl = Tile(name, t)
        self.all_tiles.append(tl)
        return tl

    def _need(self, e, k, val, out):
        kind, key = k
        if kind == "eng" and key == e and e == "pe":
            return
        kn = self.known[e]
        if kn.get(k, 0) >= val:
            return
        sem = self.sem[key] if kind == "eng" else self.dma_sems[key][0]
        out.append((sem, val))
        self.n_wait += 1
        kn[k] = val

    def _wait(self, e, k, val):
        out = []
        self._need(e, k, val, out)
        for sem, v in out:
            self.eng[e].wait_ge(sem, v)

    def _deps(self, e, reads, writes, attach=False):
        out = []
        for t in reads:
            for k, v in t.writes.items():
                self._need(e, k, v, out)
        for t in writes:
            for k, v in t.writes.items():
                self._need(e, k, v, out)
            for k, v in t.reads.items():
                self._need(e, k, v, out)
        last = out.pop() if (attach and out) else None
        for sem, v in out:
            self.eng[e].wait_ge(sem, v)
        return last

    def _record(self, ev, reads, writes):
        for t in reads:
            if t in writes:
                continue
            k = (ev[0], ev[1])
            if t.reads.get(k, 0) < ev[2]:
                t.reads[k] = ev[2]
        for t in writes:
            t.writes = {(ev[0], ev[1]): ev[2]}
            t.reads = {}

    def op(self, e, fn, reads=(), writes=()):
        rt, wt = _tiles(reads), _tiles(writes)
        last = self._deps(e, rt, wt, attach=True)
        inst = fn(self.eng[e])
        if last is not None:
            inst._wait_ge(*last)
        self.cnt[e] += 1
        self.n_inst += 1
        inst.then_inc(self.sem[e], 1)
        self._record(("eng", e, self.cnt[e]), rt, wt)

    def act(self, out, in_, func, bias=None, scale=1.0, accum=None):
        kw = {}
        if bias is not None:
            kw["bias"] = _ap(bias)
        if accum is not None:
            kw["accum_out"] = _ap(accum)
        self.op("act", lambda e: e.activation(out.ap, in_.ap, func, scale=_ap(scale), **kw),
                reads=[in_, bias, scale], writes=[out, accum])

    def tt(self, e, out, a, b, op):
        self.op(e, lambda g: g.tensor_tensor(out.ap, a.ap, b.ap, op), reads=[a, b], writes=[out])

    def stt(self, e, out, in0, scalar, in1, op0, op1):
        self.op(e, lambda g: g.scalar_tensor_tensor(out.ap, in0.ap, _ap(scalar), in1.ap, op0, op1),
                reads=[in0, scalar, in1], writes=[out])

    def ts(self, e, out, in0, s1, s2, op0, op1=None):
        if op1 is None:
            self.op(e, lambda g: g.tensor_scalar(out.ap, in0.ap, _ap(s1), None, op0), reads=[in0, s1], writes=[out])
        else:
            self.op(e, lambda g: g.tensor_scalar(out.ap, in0.ap, _ap(s1), _ap(s2), op0, op1),
                    reads=[in0, s1, s2], writes=[out])

    def copy(self, e, out, in_):
        if e == "act":
            self.op(e, lambda g: g.copy(out.ap, in_.ap), reads=[in_], writes=[out])
        else:
            self.op(e, lambda g: g.tensor_copy(out.ap, in_.ap), reads=[in_], writes=[out])

    def recip(self, out, in_):
        self.op("dve", lambda g: g.reciprocal(out.ap, in_.ap), reads=[in_], writes=[out])

    def memset(self, e, out, val):
        self.op(e, lambda g: g.memset(out.ap, val), writes=[out])

    def reduce_sum(self, out, in_):
        self.op("dve", lambda g: g.tensor_reduce(out.ap, in_.ap, mybir.AxisListType.X, ALU.add), reads=[in_], writes=[out])

    def mm(self, items):
        reads, writes = [], []
        for it in items:
            writes.append(it[0])
            reads += [it[1], it[2]]
        rt, wt = _tiles(reads), _tiles(writes)
        last = self._deps("pe", rt, wt, attach=True)
        inst = None
        for it in items:
            kw = {}
            if len(it) > 5 and it[5] is not None:
                kw["tile_position"] = it[5]
            inst = self.eng["pe"].matmul(it[0].ap, it[1].ap, it[2].ap, start=it[3], stop=it[4], **kw)
            if last is not None:
                inst._wait_ge(*last)
                last = None
            self.n_inst += 1
        self.cnt["pe"] += 1
        inst.then_inc(self.sem["pe"], 1)
        self._record(("eng", "pe", self.cnt["pe"]), rt, wt)

    def transposes(self, items, ident):
        reads = [ident] + [it[1] for it in items]
        writes = [it[0] for it in items]
        rt, wt = _tiles(reads), _tiles(writes)
        last = self._deps("pe", rt, wt, attach=True)
        inst = None
        for it in items:
            inst = self.eng["pe"].transpose(it[0].ap, it[1].ap, ident.ap)
            if last is not None:
                inst._wait_ge(*last)
                last = None
            self.n_inst += 1
        self.cnt["pe"] += 1
        inst.then_inc(self.sem["pe"], 1)
        self._record(("eng", "pe", self.cnt["pe"]), rt, wt)

    def dma(self, q, out, in_, **kw):
        rt, wt = _tiles([in_]), _tiles([out])
        self._deps(q, rt, wt)
        semname = (wt[0].name if wt else rt[0].name)
        if semname not in self.dma_sems:
            self.dma_sems[semname] = [self.nc.alloc_semaphore("d_" + semname), 0]
        ds = self.dma_sems[semname]
        ds[1] += 16
        self.eng[q].dma_start(out=_ap(out), in_=_ap(in_), **kw).then_inc(ds[0], 16)
        self.n_inst += 1
        self._record(("dma", semname, ds[1]), rt, wt)

    def _all_events(self):
        evs = [(("eng", k), v) for k, v in self.cnt.items() if v > 0]
        evs += [(("dma", k), v[1]) for k, v in self.dma_sems.items() if v[1] > 0]
        return evs

    def barrier(self, engines=ENG_NAMES):
        for e in engines:
            for k, v in self._all_events():
                self._wait(e, k, v)

    def new_epoch(self):
        self.barrier()
        self.epoch += 1
        self.sem = {e: self.nc.alloc_semaphore("prog%d_%s" % (self.epoch, e)) for e in ("pe", "act", "dve", "pool")}
        self.cnt = {e: 0 for e in self.sem}
        for e in ENG_NAMES:
            for k in [k for k in self.known[e] if k[0] == "eng"]:
                del self.known[e][k]
        for tl in self.all_tiles:
            tl.writes = {k: v for k, v in tl.writes.items() if k[0] != "eng"}
            tl.reads = {k: v for k, v in tl.reads.items() if k[0] != "eng"}

    def finish(self):
        self.barrier(engines=("sp",))


class Rot:
    def __init__(self, S, name, shape, dtype, n, psum=False):
        self.tiles = [(S.psum if psum else S.sbuf)("%s%d" % (name, i), shape, dtype) for i in range(n)]
        self.i = 0

    def next(self):
        t = self.tiles[self.i % len(self.tiles)]
        self.i += 1
        return t


D = 1024
SEQ = 4096
BATCH = 8
DEPTH = 2
EPS = 1e-6
NT = SEQ // 512
NCH = SEQ // 128
N_IN = 3360
N_INX = 3904
C_CQ, C_CKV, C_KPE, C_NAQ, C_NAK, C_NAV = 0, 256, 384, 416, 800, 1184
C_DQ, C_DK, C_DV, C_G = 1568, 1824, 2080, 2336
C_KPEP, C_DQP, C_DKP = 3360, 3392, 3648
NG = 16
MASK = -4000.0
NAB_TILES = 126
PERM_M = np.concatenate([np.arange(64), np.arange(80, 96), np.arange(64, 80)])
PERM_32 = np.concatenate([np.arange(16, 32), np.arange(0, 16)])


def _lam_init(l):
    return 0.8 - 0.6 * float(np.exp(-0.3 * l))


def _rope_tables():
    def tab(dim):
        inv = (np.float32(10000.0) ** (-(np.arange(0, dim, 2, dtype=np.float32)) / np.float32(dim))).astype(np.float32)
        ang = (np.arange(SEQ, dtype=np.float32)[:, None] * inv[None, :]).astype(np.float32)
        return np.cos(ang.astype(np.float64)).astype(np.float32).T, np.sin(ang.astype(np.float64)).astype(np.float32).T
    cm, sm = tab(32)
    cosm = np.ones((96, SEQ), np.float32)
    sinm = np.zeros((96, SEQ), np.float32)
    cosm[64:80] = cm
    cosm[80:96] = cm
    sinm[64:80] = -sm
    sinm[80:96] = sm
    cd, sd = tab(32)
    cosd = np.tile(cd, (8, 1))
    sind = np.tile(np.concatenate([-sd, sd], 0), (4, 1))
    return cosm, sinm, cosd.astype(np.float32), sind.astype(np.float32)


def _na_chunks(j):
    if j <= 1:
        return [0, 1, 2, 3]
    if j >= 30:
        return [28, 29, 30, 31]
    return [j - 2, j - 1, j, j + 1, j + 2]


def _nab_index(h, j, i):
    if 2 <= j <= 29:
        return h * 5 + (i - j + 2)
    jb = j if j <= 1 else j - 28
    ii = i if j <= 1 else i - 28
    return 30 + (h * 4 + jb) * 4 + ii


def _na_bias_tables(rpb):
    out = np.full((128, NAB_TILES, 128), MASK, np.float32)
    a = np.arange(2)[:, None]
    cc = np.arange(64)[None, :]
    pairs = [(10, 10 + d) for d in range(-2, 3)] + [(j, i) for j in (0, 1, 30, 31) for i in _na_chunks(j)]
    for (j, i) in pairs:
        rk = (2 * i + a + 0 * cc).reshape(128)
        ck = (0 * a + cc).reshape(128)
        rq = (2 * j + a + 0 * cc).reshape(128)
        cq = ck
        r0 = np.clip(rq - 4, 0, 56)
        c0 = np.clip(cq - 8, 0, 48)
        vr = (rk[:, None] >= r0[None, :]) & (rk[:, None] < r0[None, :] + 8)
        vc = (ck[:, None] >= c0[None, :]) & (ck[:, None] < c0[None, :] + 16)
        valid = vr & vc
        dr = np.clip(rk[:, None] - rq[None, :] + 7, 0, 14)
        dc = np.clip(ck[:, None] - cq[None, :] + 15, 0, 30)
        for h in range(6):
            t = np.where(valid, rpb[h][dr, dc], np.float32(MASK)).astype(np.float32)
            out[:, _nab_index(h, j, i), :] = t
    return out


def _prep_layer_inputs(inp):
    L = DEPTH
    f32 = np.float32
    o = {}
    o["ada_w_r"] = np.ascontiguousarray(inp["ada_w"].reshape(L, 8, 128, 3 * D).transpose(0, 2, 1, 3))
    o["ada_b"] = np.ascontiguousarray(inp["ada_b"].reshape(L, 1, 3 * D))
    o["norm_g"] = np.ascontiguousarray(inp["norm_g"])
    w_in = inp["w_in"]
    ext = np.concatenate([
        w_in,
        w_in[:, :, C_KPE + PERM_32],
        w_in[:, :, C_DQ + (np.arange(256) // 32) * 32 + PERM_32[np.arange(256) % 32]],
        w_in[:, :, C_DK + (np.arange(256) // 32) * 32 + PERM_32[np.arange(256) % 32]],
    ], axis=2)
    o["w_in_r"] = np.ascontiguousarray(ext.reshape(L, 8, 128, N_INX).transpose(0, 2, 1, 3))
    w_uq = inp["w_uq"]
    permq = (np.arange(576) // 96) * 96 + PERM_M[np.arange(576) % 96]
    uq = np.concatenate([w_uq, w_uq[:, :, permq]], axis=2)
    o["w_uq_r"] = np.ascontiguousarray(uq.reshape(L, 2, 128, 1152).transpose(0, 2, 1, 3))
    w_ukv = inp["w_ukv"].reshape(L, 128, 6, 128)
    wk = np.zeros((L, 128, 6, 96), f32)
    wk[:, :, :, 0:64] = w_ukv[:, :, :, 0:64]
    o["wk_r"] = wk
    o["wv_r"] = np.ascontiguousarray(w_ukv[:, :, :, 64:128].reshape(L, 128, 384))
    o["w_out_r"] = np.ascontiguousarray(inp["w_out"].reshape(L, 8, 128, D).transpose(0, 2, 1, 3))
    g = np.zeros((L, 128, NG), f32)
    g[:, :, 0] = inp["q_lat_g"][:, 0:128]
    g[:, :, 1] = inp["q_lat_g"][:, 128:256]
    g[:, :, 2] = inp["kv_lat_g"]
    g[:, 0:96, 3] = inp["mla_q_g"]
    g[:, 0:96, 4] = inp["mla_q_g"][:, PERM_M]
    g[:, 0:96, 5] = inp["mla_k_g"]
    g[:, 0:96, 6] = inp["mla_k_g"][:, PERM_M]
    g[:, :, 7] = np.tile(inp["na_q_g"], (1, 2))
    g[:, :, 8] = np.tile(inp["na_k_g"], (1, 2))
    g[:, :, 9] = np.tile(inp["diff_q_g"], (1, 4))
    g[:, :, 10] = np.tile(inp["diff_q_g"][:, PERM_32], (1, 4))
    g[:, :, 11] = np.tile(inp["diff_k_g"], (1, 4))
    g[:, :, 12] = np.tile(inp["diff_k_g"][:, PERM_32], (1, 4))
    g[:, 0:64, 13] = inp["subln_g"]
    o["gains"] = g
    o["lamv"] = np.ascontiguousarray(np.stack([inp["lam_q1"], inp["lam_k1"], inp["lam_q2"], inp["lam_k2"]], axis=1).reshape(L, 128))
    o["nab"] = np.stack([_na_bias_tables(inp["na_rpb"][l]) for l in range(L)], axis=0)
    return {k: np.ascontiguousarray(v, dtype=f32) for k, v in o.items()}


def _const_inputs():
    cosm, sinm, cosd, sind = _rope_tables()
    bf = ml_dtypes.bfloat16
    esel = np.zeros((32, 96), np.float32)
    esel[np.arange(32), 64 + np.arange(32)] = 1.0
    bd32 = np.kron(np.eye(4, dtype=np.float32), np.ones((32, 32), np.float32))
    bd64 = np.kron(np.eye(2, dtype=np.float32), np.ones((64, 64), np.float32))
    return {
        "cosm": cosm, "sinm": sinm, "cosd": cosd, "sind": sind,
        "ident": np.eye(128, dtype=np.float32).astype(bf),
        "esel": esel.astype(bf), "bd32": bd32.astype(bf), "bd64": bd64.astype(bf),
        "dmask": np.kron(np.eye(4, dtype=np.float32), np.ones((32, 1), np.float32)),
    }


def _dram_in(nc, name, shape, dtype=F32):
    return nc.dram_tensor(name, list(shape), dtype, kind="ExternalInput").ap()


def build_program(layers, n_layers_total=DEPTH):
    nc = bass.Bass("TRN2", target_bir_lowering=False)
    S = Sched(nc)
    L = n_layers_total
    x_in = _dram_in(nc, "x", [SEQ, D])
    c_r = _dram_in(nc, "c_r", [128, 8])
    W = {
        "ada_w_r": _dram_in(nc, "ada_w_r", [L, 128, 8, 3 * D]),
        "ada_b": _dram_in(nc, "ada_b", [L, 1, 3 * D]),
        "norm_g": _dram_in(nc, "norm_g", [L, D]),
        "w_in_r": _dram_in(nc, "w_in_r", [L, 128, 8, N_INX]),
        "w_uq_r": _dram_in(nc, "w_uq_r", [L, 128, 2, 1152]),
        "wk_r": _dram_in(nc, "wk_r", [L, 128, 6, 96]),
        "wv_r": _dram_in(nc, "wv_r", [L, 128, 384]),
        "w_out_r": _dram_in(nc, "w_out_r", [L, 128, 8, D]),
        "gains": _dram_in(nc, "gains", [L, 128, NG]),
        "lamv": _dram_in(nc, "lamv", [L, 128]),
        "nab": _dram_in(nc, "nab", [L, 128, NAB_TILES, 128]),
        "cosm": _dram_in(nc, "cosm", [96, SEQ]),
        "sinm": _dram_in(nc, "sinm", [96, SEQ]),
        "cosd": _dram_in(nc, "cosd", [128, SEQ]),
        "sind": _dram_in(nc, "sind", [128, SEQ]),
        "ident": _dram_in(nc, "ident", [128, 128], BF16),
        "esel": _dram_in(nc, "esel", [32, 96], BF16),
        "bd32": _dram_in(nc, "bd32", [128, 128], BF16),
        "bd64": _dram_in(nc, "bd64", [128, 128], BF16),
        "dmask": _dram_in(nc, "dmask", [128, 4]),
    }
    y_out = nc.dram_tensor("y", [SEQ, D], F32, kind="ExternalOutput").ap()
    scr = {
        "qT_mla": nc.dram_tensor("s_qT_mla", [6, 96, SEQ], BF16).ap(),
        "kT_mla": nc.dram_tensor("s_kT_mla", [6, 96, SEQ], BF16).ap(),
        "v_mla": nc.dram_tensor("s_v_mla", [SEQ, 390], BF16).ap(),
        "qT_na": nc.dram_tensor("s_qT_na", [3, 128, SEQ], BF16).ap(),
        "kT_na": nc.dram_tensor("s_kT_na", [3, 128, SEQ], BF16).ap(),
        "v_na": nc.dram_tensor("s_v_na", [SEQ, 390], BF16).ap(),
        "qT_d": nc.dram_tensor("s_qT_d", [2, 128, SEQ], BF16).ap(),
        "kT_d": nc.dram_tensor("s_kT_d", [2, 128, SEQ], BF16).ap(),
        "v_d": nc.dram_tensor("s_v_d", [SEQ, 260], BF16).ap(),
        "sgT": nc.dram_tensor("s_sgT", [D, SEQ], BF16).ap(),
        "mixT": nc.dram_tensor("s_mixT", [D, SEQ], BF16).ap(),
    }
    xmid = [nc.dram_tensor("s_x%d" % i, [SEQ, D], F32).ap() for i in range(max(0, len(layers) - 1))]

    S.push()
    K = {}
    K["ident"] = S.sbuf("ident", [128, 128], BF16)
    K["esel"] = S.sbuf("esel", [32, 96], BF16)
    K["bd32"] = S.sbuf("bd32", [128, 128], BF16)
    K["bd64"] = S.sbuf("bd64", [128, 128], BF16)
    K["ones_b"] = S.sbuf("ones_b", [128, 128], BF16)
    K["ones_f"] = S.sbuf("ones_f", [128, 128], F32)
    K["eps"] = S.sbuf("eps", [128, 1], F32)
    for nm in ("ident", "esel", "bd32", "bd64"):
        S.dma("sp", K[nm][:], W[nm][:, :])
    S.memset("dve", K["ones_b"][:], 1.0)
    S.memset("dve", K["ones_f"][:], 1.0)
    S.memset("dve", K["eps"][:], EPS)

    for li, l in enumerate(layers):
        xin = x_in if li == 0 else xmid[li - 1]
        xout = y_out if li == len(layers) - 1 else xmid[li]
        if li > 0:
            S.new_epoch()
        _layer(nc, S, K, W, scr, l, xin, xout, c_r)
    S.finish()
    S.scopes.pop()
    return nc, S


def _layer(nc, S, K, W, scr, l, xin, xout, c_r):
    S.push()
    gains = S.sbuf("gains", [128, NG], F32)
    nlam = S.sbuf("nlam", [128, 1], F32)
    woutg = S.sbuf("woutg", [128, 8, D], BF16)
    S.dma("sp", gains[:], W["gains"][l])

    S.push()
    Gb = S.sbuf("Gb", [128, D], F32)
    Shb = S.sbuf("Shb", [128, D], F32)
    Wf = S.sbuf("Wf", [128, 8, N_INX], BF16)
    Wuq = S.sbuf("Wuq", [128, 2, 1152], BF16)
    Wk = S.sbuf("Wk", [128, 6, 96], BF16)
    Wv = S.sbuf("Wv", [128, 384], BF16)

    S.push()
    PS0 = Rot(S, "ps0_", [128, 512], F32, 2, psum=True)
    c_sb = S.sbuf("c_sb", [128, 8], F32)
    sc = S.sbuf("sc", [128, 8], F32)
    screp = S.sbuf("screp", [128, 8, 128], F32)
    adab = S.sbuf("adab", [1, 3 * D], F32)
    modb = S.sbuf("modb", [128, 3 * D], F32)
    ngb = S.sbuf("ngb", [128, D], F32)
    lamb = S.sbuf("lamb", [128, 128], F32)
    lt = S.sbuf("lt", [128, 64], F32)
    ls = S.sbuf("ls", [128, 2], F32)
    le = S.sbuf("le", [128, 2], F32)
    STG = Rot(S, "stg", [128, 4096], F32, 2)
    S.dma("sp", c_sb[:], c_r[:, :])
    S.dma("sp", adab[:], W["ada_b"][l])
    S.dma("sp", ngb[:], W["norm_g"][l].partition_broadcast(128))
    S.dma("sp", lamb[:], W["lamv"][l].partition_broadcast(128))
    S.act(sc[:], c_sb[:], AF.Silu)
    for k in range(8):
        S.ts("dve", screp[:, k, :], K["ones_f"][:], sc[:, k:k + 1], None, ALU.mult)
    for n in range(6):
        st = STG.next()
        S.dma("sp", st[:].rearrange("p (k n) -> p k n", k=8), W["ada_w_r"][l][:, :, n * 512:(n + 1) * 512])
        ps = PS0.next()
        stv = st[:].rearrange("p (k n) -> p k n", k=8)
        items = [(ps[:], screp[:, k, :], stv[:, k, :], k == 0, False) for k in range(8)]
        items.append((ps[:], K["ones_f"][0:1, :], adab[0:1, n * 512:(n + 1) * 512], False, True))
        S.mm(items)
        S.copy("dve", modb[:, n * 512:(n + 1) * 512], ps[:])
    S.copy("dve", Shb[:], modb[:, 0:D])
    S.stt("dve", Gb[:], modb[:, D:2 * D], 1.0, ngb[:], ALU.add, ALU.mult)
    S.tt("dve", lt[:, 0:32], lamb[:, 0:32], lamb[:, 32:64], ALU.mult)
    S.tt("dve", lt[:, 32:64], lamb[:, 64:96], lamb[:, 96:128], ALU.mult)
    S.reduce_sum(ls[:, 0:1], lt[:, 0:32])
    S.reduce_sum(ls[:, 1:2], lt[:, 32:64])
    S.act(le[:], ls[:], AF.Exp)
    S.tt("dve", nlam[:], le[:, 1:2], le[:, 0:1], ALU.subtract)
    S.ts("dve", nlam[:], nlam[:], -_lam_init(l), None, ALU.add)
    S.ts("dve", gains[:, 13:14], gains[:, 13:14], 1.0 - _lam_init(l), None, ALU.mult)
    for k in range(8):
        st = STG.next()
        S.dma("sp", st[:, 0:N_INX], W["w_in_r"][l][:, k, :])
        S.copy("dve" if k % 2 == 0 else "pool", Wf[:, k, :], st[:, 0:N_INX])
    st = STG.next()
    S.dma("sp", st[:, 0:2304].rearrange("p (k n) -> p k n", k=2), W["w_uq_r"][l])
    S.copy("dve", Wuq[:], st[:, 0:2304].rearrange("p (k n) -> p k n", k=2))
    st = STG.next()
    S.dma("sp", st[:, 0:576].rearrange("p (h n) -> p h n", h=6), W["wk_r"][l])
    S.dma("sp", st[:, 1024:1408], W["wv_r"][l])
    S.copy("dve", Wk[:], st[:, 0:576].rearrange("p (h n) -> p h n", h=6))
    S.copy("dve", Wv[:], st[:, 1024:1408])
    for half in range(2):
        st = STG.next()
        S.dma("sp", st[:].rearrange("p (c d) -> p c d", c=4), W["w_out_r"][l][:, half * 4:(half + 1) * 4, :])
        for c in range(4):
            S.tt("dve" if c % 2 == 0 else "pool", woutg[:, half * 4 + c, :], st[:, c * D:(c + 1) * D], modb[:, 2 * D:3 * D], ALU.mult)
    S.pop()

    _phase_a(nc, S, K, W, scr, l, xin, gains, Gb, Shb, Wf, Wuq, Wk, Wv)
    S.pop()

    _phase_attn(nc, S, K, W, scr, l, gains, nlam)
    _phase_c(nc, S, K, scr, xin, xout, woutg)
    S.pop()


def _phase_a(nc, S, K, W, scr, l, xin, gains, Gb, Shb, Wf, Wuq, Wk, Wv):
    S.push()
    XT = Rot(S, "xt", [128, D], F32, 2)
    junk = S.sbuf("junk", [128, D], BF16)
    st1 = Rot(S, "st1_", [128, 4], F32, 2)
    hn = S.sbuf("hn", [128, D], F32)
    hb = Rot(S, "hb", [128, D], BF16, 2)
    HT = Rot(S, "hT", [128, 8, 512], BF16, 2)
    COSM = Rot(S, "cosm", [96, 512], F32, 2)
    SINM = Rot(S, "sinm", [96, 512], F32, 2)
    COSD = Rot(S, "cosd", [128, 512], F32, 2)
    SIND = Rot(S, "sind", [128, 512], F32, 2)
    SQ = Rot(S, "sq", [128, 512], BF16, 3)
    SD = Rot(S, "sd", [128, 512], F32, 2)
    RS = Rot(S, "rs", [128, 512], F32, 2)
    TA = Rot(S, "ta", [128, 512], F32, 2)
    TB = Rot(S, "tb", [128, 512], F32, 2)
    TC = Rot(S, "tc", [128, 512], F32, 2)
    OB = Rot(S, "ob", [128, 512], BF16, 4)
    cqn = S.sbuf("cqn", [128, 2, 512], BF16)
    ckvn = S.sbuf("ckvn", [128, 512], BF16)
    kpe = S.sbuf("kpe", [32, 2, 512], BF16)
    VS = Rot(S, "vs", [128, 6, 65], BF16, 3)
    psT = S.psum("psT", [128, D], BF16)
    PA = Rot(S, "pa", [128, 512], F32, 3, psum=True)
    PB = Rot(S, "pb", [128, 512], F32, 2, psum=True)
    PSS = Rot(S, "pss", [128, 512], F32, 2, psum=True)
    for t in VS.tiles:
        S.memset("pool", t[:], 1.0)

    def rstd_from(ps_list, M, ones, n):
        sqs = []
        for ps in ps_list:
            sq = SQ.next()
            S.act(sq[0:M, :], ps[0:M, :], AF.Square)
            sqs.append(sq)
        pss = PSS.next()
        S.mm([(pss[0:M, :], ones[0:M, 0:M], sq[0:M, :], i == 0, i == len(sqs) - 1) for i, sq in enumerate(sqs)])
        sd = SD.next()
        S.act(sd[0:M, :], pss[0:M, :], AF.Sqrt, bias=K["eps"][0:M, 0:1], scale=1.0 / n)
        rs = RS.next()
        S.recip(rs[0:M, :], sd[0:M, :])
        return rs

    def proj(cols, M, ht):
        ps = PA.next()
        S.mm([(ps[0:M, :], Wf[:, k, cols:cols + M], ht[:, k, :], k == 0, k == 7) for k in range(8)])
        return ps

    def projb(cols, M, ht):
        ps = PB.next()
        S.mm([(ps[0:M, :], Wf[:, k, cols:cols + M], ht[:, k, :], k == 0, k == 7) for k in range(8)])
        return ps

    def rope_finish(ps, psp, M, gcol, gpcol, cos, sin, rs, dst):
        ta, tb, tc = TA.next(), TB.next(), TC.next()
        S.stt("dve", ta[0:M, :], ps[0:M, :], gains[0:M, gcol:gcol + 1], cos[0:M, :], ALU.mult, ALU.mult)
        S.stt("dve", tb[0:M, :], psp[0:M, :], gains[0:M, gpcol:gpcol + 1], sin[0:M, :], ALU.mult, ALU.mult)
        S.tt("pool", tc[0:M, :], ta[0:M, :], tb[0:M, :], ALU.add)
        ob = OB.next()
        S.tt("pool", ob[0:M, :], tc[0:M, :], rs[0:M, :], ALU.mult)
        S.dma("pool", dst, ob[0:M, :])

    def load_x(T, s):
        xt = XT.next()
        r0 = T * 512 + s * 128
        S.dma("sp", xt[:], xin[r0:r0 + 128, :])
        return xt

    xq = [load_x(0, 0)]
    for T in range(NT):
        t0 = T * 512
        cosm, sinm, cosd, sind = COSM.next(), SINM.next(), COSD.next(), SIND.next()
        S.dma("sp", cosm[:], W["cosm"][:, t0:t0 + 512])
        S.dma("sp", sinm[:], W["sinm"][:, t0:t0 + 512])
        S.dma("sp", cosd[:], W["cosd"][:, t0:t0 + 512])
        S.dma("sp", sind[:], W["sind"][:, t0:t0 + 512])
        ht = HT.next()
        for s in range(4):
            xt = xq.pop(0)
            nxt = (T, s + 1) if s < 3 else (T + 1, 0)
            if nxt[0] < NT:
                xq.append(load_x(*nxt))
            st = st1.next()
            S.act(junk[:], xt[:], AF.Square, accum=st[:, 0:1])
            S.act(st[:, 1:2], st[:, 0:1], AF.Sqrt, bias=K["eps"][:, 0:1], scale=1.0 / D)
            S.recip(st[:, 2:3], st[:, 1:2])
            S.stt("dve", hn[:], xt[:], st[:, 2:3], Gb[:], ALU.mult, ALU.mult)
            h = hb.next()
            S.tt("pool", h[:], hn[:], Shb[:], ALU.add)
            S.transposes([(psT[:, k * 128:(k + 1) * 128], h[:, k * 128:(k + 1) * 128]) for k in range(8)], K["ident"][:])
            S.copy("dve", ht[:, :, s * 128:(s + 1) * 128], psT[:].rearrange("p (k t) -> p k t", k=8))

        pc = [proj(C_CQ, 128, ht), proj(C_CQ + 128, 128, ht)]
        rs = rstd_from(pc, 128, K["ones_b"], 256)
        for k in range(2):
            S.stt("dve", cqn[:, k, :], pc[k][:], gains[:, k:k + 1], rs[:], ALU.mult, ALU.mult)
        for h in range(6):
            ps, psp = PA.next(), PB.next()
            S.mm([(ps[0:96, :], Wuq[:, k, h * 96:(h + 1) * 96], cqn[:, k, :], k == 0, k == 1) for k in range(2)])
            S.mm([(psp[0:96, :], Wuq[:, k, 576 + h * 96:576 + (h + 1) * 96], cqn[:, k, :], k == 0, k == 1) for k in range(2)])
            rs = rstd_from([ps], 96, K["ones_b"], 96)
            rope_finish(ps, psp, 96, 3, 4, cosm, sinm, rs, scr["qT_mla"][h, :, t0:t0 + 512])

        pk = proj(C_CKV, 128, ht)
        rs = rstd_from([pk], 128, K["ones_b"], 128)
        S.stt("dve", ckvn[:], pk[:], gains[:, 2:3], rs[:], ALU.mult, ALU.mult)
        pe1 = proj(C_KPE, 32, ht)
        S.copy("act", kpe[:, 0, :], pe1[0:32, :])
        pe2 = projb(C_KPEP, 32, ht)
        S.copy("act", kpe[:, 1, :], pe2[0:32, :])
        for h in range(6):
            ps, psp = PA.next(), PB.next()
            S.mm([(ps[0:96, :], Wk[:, h, :], ckvn[:], True, False), (ps[0:96, :], K["esel"][:], kpe[:, 0, :], False, True)])
            S.mm([(psp[0:96, :], Wk[:, h, :], ckvn[:], True, False), (psp[0:96, :], K["esel"][:], kpe[:, 1, :], False, True)])
            rs = rstd_from([ps], 96, K["ones_b"], 96)
            rope_finish(ps, psp, 96, 5, 6, cosm, sinm, rs, scr["kT_mla"][h, :, t0:t0 + 512])
        for s in range(4):
            ps = PA.next()
            S.mm([(ps[:, 0:384], ckvn[:, s * 128:(s + 1) * 128], Wv[:], True, True)])
            vs = VS.next()
            S.copy("dve", vs[:, :, 0:64], ps[:, 0:384].rearrange("p (h d) -> p h d", h=6))
            S.dma("pool", scr["v_mla"][t0 + s * 128:t0 + (s + 1) * 128, :], vs[:].rearrange("p h d -> p (h d)"))

        for (c0, gcol, dst) in ((C_NAQ, 7, "qT_na"), (C_NAK, 8, "kT_na")):
            for c in range(3):
                ps = proj(c0 + c * 128, 128, ht)
                rs = rstd_from([ps], 128, K["bd64"], 64)
                ob = OB.next()
                S.stt("dve", ob[:], ps[:], gains[:, gcol:gcol + 1], rs[:], ALU.mult, ALU.mult)
                S.dma("pool", scr[dst][c, :, t0:t0 + 512], ob[:])
        for s in range(4):
            ps = PA.next()
            S.mm([(ps[:, 0:384], ht[:, k, s * 128:(s + 1) * 128], Wf[:, k, C_NAV:C_NAV + 384], k == 0, k == 7) for k in range(8)])
            vs = VS.next()
            S.copy("dve", vs[:, :, 0:64], ps[:, 0:384].rearrange("p (h d) -> p h d", h=6))
            S.dma("pool", scr["v_na"][t0 + s * 128:t0 + (s + 1) * 128, :], vs[:].rearrange("p h d -> p (h d)"))

        for (c0, cp, gcol, dst) in ((C_DQ, C_DQP, 9, "qT_d"), (C_DK, C_DKP, 11, "kT_d")):
            for c in range(2):
                ps = proj(c0 + c * 128, 128, ht)
                psp = projb(cp + c * 128, 128, ht)
                rs = rstd_from([ps], 128, K["bd32"], 32)
                rope_finish(ps, psp, 128, gcol, gcol + 1, cosd, sind, rs, scr[dst][c, :, t0:t0 + 512])
        for s in range(4):
            ps = PA.next()
            S.mm([(ps[:, 0:256], ht[:, k, s * 128:(s + 1) * 128], Wf[:, k, C_DV:C_DV + 256], k == 0, k == 7) for k in range(8)])
            vs = VS.next()
            S.copy("dve", vs[:, 0:4, 0:64], ps[:, 0:256].rearrange("p (h d) -> p h d", h=4))
            S.dma("pool", scr["v_d"][t0 + s * 128:t0 + (s + 1) * 128, :], vs[:, 0:4, :].rearrange("p h d -> p (h d)"))

        for c in range(8):
            ps = proj(C_G + c * 128, 128, ht)
            ob = OB.next()
            S.act(ob[:], ps[:], AF.Silu)
            S.dma("pool", scr["sgT"][c * 128:(c + 1) * 128, t0:t0 + 512], ob[:])
    S.pop()


def _phase_attn(nc, S, K, W, scr, l, gains, nlam):
    S.push()
    PSS = Rot(S, "psS", [128, 1024], F32, 2, psum=True)
    PSO = Rot(S, "psO", [128, 512], F32, 2, psum=True)
    psB = S.psum("psB", [128, 512], F32)
    psN = S.psum("psN", [128, 512], F32)
    PT = Rot(S, "pt", [128, 1024], BF16, 2)
    OS = Rot(S, "osb", [128, 512], F32, 2)
    rrow = S.sbuf("rrow", [128, 512], F32)
    UU = Rot(S, "uu", [64, 512], F32, 3)
    SG = Rot(S, "sg", [64, 512], BF16, 2)
    MX = Rot(S, "mx", [64, 512], BF16, 2)

    def epilogue(pso):
        osb = OS.next()
        S.copy("dve", osb[0:65, :], pso[0:65, :])
        S.recip(rrow[64:65, :], osb[64:65, :])
        S.mm([(psB[0:64, :], K["ones_f"][64:65, 0:64], rrow[64:65, :], True, True)])
        u = UU.next()
        S.tt("dve", u[:], osb[0:64, :], psB[0:64, :], ALU.mult)
        return u

    def load_sg(row0, t0):
        sg = SG.next()
        S.dma("sp", sg[:], scr["sgT"][row0:row0 + 64, t0:t0 + 512])
        return sg

    def store_mix(u_like, sg, row0, t0, eng="pool"):
        mx = MX.next()
        S.tt(eng, mx[:], u_like[:], sg[:], ALU.mult)
        S.dma("pool", scr["mixT"][row0:row0 + 64, t0:t0 + 512], mx[:])

    def dense_unit(kt, qt, vfn, scale, tp):
        pso = PSO.next()
        NP = NCH // 2
        cur = {}

        def emit_s(j):
            pss = PSS.next()
            cur[j] = pss
            S.mm([(pss[:, 0:512], kt(2 * j), qt, True, True, tp), (pss[:, 512:1024], kt(2 * j + 1), qt, True, True, tp)])

        emit_s(0)
        for j in range(NP):
            if j + 1 < NP:
                emit_s(j + 1)
            p = PT.next()
            S.act(p[:], cur.pop(j)[:], AF.Exp, scale=scale)
            S.mm([(pso[0:65, :], vfn(2 * j), p[:, 0:512], j == 0, False),
                  (pso[0:65, :], vfn(2 * j + 1), p[:, 512:1024], False, j == NP - 1)])
        return epilogue(pso)

    S.push()
    Vt = S.sbuf("Vt", [128, NCH, 390], BF16)
    KT = Rot(S, "KT", [96, SEQ], BF16, 2)
    QT = Rot(S, "QT", [96, 512], BF16, 2)
    S.dma("sp", Vt[:], scr["v_mla"].rearrange("(c p) f -> p c f", p=128))
    for h in range(6):
        kt = KT.next()
        S.dma("sp", kt[:], scr["kT_mla"][h])
        for T in range(NT):
            t0 = T * 512
            qt = QT.next()
            S.dma("sp", qt[:], scr["qT_mla"][h, :, t0:t0 + 512])
            sg = load_sg(h * 64, t0)
            u = dense_unit(lambda kc, kt=kt: kt[:, kc * 128:(kc + 1) * 128], qt[:],
                           lambda kc, h=h: Vt[:, kc, h * 65:(h + 1) * 65], 96.0 ** -0.5, None)
            store_mix(u, sg, h * 64, t0)
    S.pop()

    S.push()
    Vd = S.sbuf("Vd", [128, NCH, 260], BF16)
    KTd = Rot(S, "KTd", [128, SEQ], BF16, 2)
    QTd = Rot(S, "QTd", [128, 512], BF16, 2)
    KM = [S.sbuf("KM%d" % i, [128, SEQ], BF16) for i in range(4)]
    dmask = S.sbuf("dmask", [128, 4], F32)
    S.dma("sp", dmask[:], W["dmask"][:, :])
    dd = S.sbuf("dd", [64, 512], F32)
    dsq = S.sbuf("dsq", [64, 512], BF16)
    dsd = S.sbuf("dsd", [64, 512], F32)
    drs = S.sbuf("drs", [64, 512], F32)
    dm = S.sbuf("dm", [64, 512], F32)
    S.dma("sp", Vd[:], scr["v_d"].rearrange("(c p) f -> p c f", p=128))
    for c in range(2):
        kt = KTd.next()
        S.dma("sp", kt[:], scr["kT_d"][c])
        for sub in range(4):
            S.ts("dve", KM[sub][:], kt[:], dmask[:, sub:sub + 1], None, ALU.mult)
        for T in range(NT):
            t0 = T * 512
            qt = QTd.next()
            S.dma("sp", qt[:], scr["qT_d"][c, :, t0:t0 + 512])
            for hh in range(2):
                h = 2 * c + hh
                sg = load_sg(768 + h * 64, t0)
                us = []
                for n in range(2):
                    km = KM[hh * 2 + n]
                    us.append(dense_unit(lambda kc, km=km: km[:, kc * 128:(kc + 1) * 128], qt[:],
                                         lambda kc, h=h: Vd[:, kc, h * 65:(h + 1) * 65], 32.0 ** -0.5, None))
                S.stt("dve", dd[:], us[1][:], nlam[0:64, 0:1], us[0][:], ALU.mult, ALU.add)
                S.tt("pool", dsq[:], dd[:], dd[:], ALU.mult)
                S.mm([(psN[0:64, :], K["ones_b"][0:64, 0:64], dsq[:], True, True)])
                S.act(dsd[:], psN[0:64, :], AF.Sqrt, bias=K["eps"][0:64, 0:1], scale=1.0 / 64)
                S.recip(drs[:], dsd[:])
                S.stt("dve", dm[:], dd[:], gains[0:64, 13:14], drs[:], ALU.mult, ALU.mult)
                store_mix(dm, sg, 768 + h * 64, t0)
    S.pop()

    S.push()
    Vn = S.sbuf("Vn", [128, NCH, 390], BF16)
    KTn = S.sbuf("KTn", [128, 3, SEQ], BF16)
    QTn = S.sbuf("QTn", [128, 3, SEQ], BF16)
    nabb = S.sbuf("nabb", [128, NAB_TILES, 128], BF16)
    NST = Rot(S, "nst", [128, 14, 128], F32, 2)
    S.dma("sp", Vn[:], scr["v_na"].rearrange("(c p) f -> p c f", p=128))
    for c in range(3):
        S.dma("sp", KTn[:, c, :], scr["kT_na"][c])
        S.dma("sp", QTn[:, c, :], scr["qT_na"][c])
    for g in range(NAB_TILES // 14):
        st = NST.next()
        S.dma("sp", st[:], W["nab"][l][:, g * 14:(g + 1) * 14, :])
        S.ts("dve" if g % 2 == 0 else "pool", nabb[:, g * 14:(g + 1) * 14, :], st[:], 8.0, None, ALU.mult)
    for J in range(NT):
        t0 = J * 512
        for h in range(6):
            c, b = h // 2, (h % 2) * 64
            sg = load_sg(384 + h * 64, t0)
            pso = PSO.next()
            subs = []
            for jj in range(4):
                j = 4 * J + jj
                subs.append((jj, j, _na_chunks(j)))
            cur = {}

            def emit_s(idx):
                jj, j, chunks = subs[idx]
                pss = PSS.next()
                cur[idx] = pss
                items = []
                for ci, i in enumerate(chunks):
                    o = pss[:, ci * 128:(ci + 1) * 128]
                    items.append((o, KTn[b:b + 64, c, i * 128:(i + 1) * 128], QTn[b:b + 64, c, j * 128:(j + 1) * 128], True, False))
                    items.append((o, K["ident"][:], nabb[:, _nab_index(h, j, i), :], False, True))
                S.mm(items)

            emit_s(0)
            for idx in range(4):
                if idx + 1 < 4:
                    emit_s(idx + 1)
                jj, j, chunks = subs[idx]
                n = len(chunks)
                p = PT.next()
                S.act(p[:, 0:n * 128], cur.pop(idx)[:, 0:n * 128], AF.Exp, scale=0.125)
                S.mm([(pso[0:65, jj * 128:(jj + 1) * 128], Vn[:, i, h * 65:(h + 1) * 65], p[:, ci * 128:(ci + 1) * 128], ci == 0, ci == n - 1)
                      for ci, i in enumerate(chunks)])
            u = epilogue(pso)
            store_mix(u, sg, 384 + h * 64, t0)
    S.pop()
    S.pop()


def _phase_c(nc, S, K, scr, xin, xout, woutg):
    S.push()
    MT = Rot(S, "mt", [128, 8, 512], BF16, 2)
    XC = Rot(S, "xc", [128, D], F32, 2)
    OC = Rot(S, "oc", [128, D], F32, 2)
    PY = Rot(S, "py", [128, 512], F32, 4, psum=True)
    for T in range(NT):
        t0 = T * 512
        mt = MT.next()
        S.dma("sp", mt[:], scr["mixT"][:, t0:t0 + 512].rearrange("(c p) t -> p c t", p=128))
        for s in range(4):
            r0 = t0 + s * 128
            xc = XC.next()
            S.dma("sp", xc[:], xin[r0:r0 + 128, :])
            oc = OC.next()
            for n in range(2):
                py = PY.next()
                S.mm([(py[:], mt[:, c, s * 128:(s + 1) * 128], woutg[:, c, n * 512:(n + 1) * 512], c == 0, c == 7) for c in range(8)])
                S.tt("dve", oc[:, n * 512:(n + 1) * 512], py[:], xc[:, n * 512:(n + 1) * 512], ALU.add)
            S.dma("pool", xout[r0:r0 + 128, :], oc[:])
    S.pop()


_PROG_CACHE = {}


def _get_program(layers):
    key = tuple(layers)
    if key not in _PROG_CACHE:
        _PROG_CACHE[key] = build_program(list(layers))[0]
    return _PROG_CACHE[key]


FUSED = True


def kernel(**inputs):
    inp = {k: np.asarray(v, dtype=np.float32) for k, v in inputs.items()}
    shared = _prep_layer_inputs(inp)
    shared.update(_const_inputs())
    x = inp["x"]
    c = inp["c"]
    c_r = [np.ascontiguousarray(c[b].reshape(8, 128).T) for b in range(BATCH)]
    plan = [list(range(DEPTH))] if FUSED else [[l] for l in range(DEPTH)]
    cur = [np.ascontiguousarray(x[b]) for b in range(BATCH)]
    for layers in plan:
        nc = _get_program(layers)
        in_maps = []
        for b in range(BATCH):
            m = dict(shared)
            m["x"] = cur[b]
            m["c_r"] = c_r[b]
            in_maps.append(m)
        res = run_bass_kernel_spmd(nc, in_maps, core_ids=list(range(BATCH)))
        cur = [np.asarray(res.results[b]["y"], dtype=np.float32) for b in range(BATCH)]
    return np.stack(cur, axis=0)
```
